# Optimizing a Trainium2 kernel written in Bass

```python
import math
import jax, jax.numpy as jnp
from jax import lax
import numpy as np

D_MODEL = 4096
BATCH = 2
SEQ = 4096
DEPTH = 1

D_MIX = D_MODEL
MLA_HEADS = 16
QK_NOPE_DIM = 128
QK_ROPE_DIM = 64
QK_HEAD_DIM = QK_NOPE_DIM + QK_ROPE_DIM
V_HEAD_DIM = 128
Q_LORA_RANK = 1024
KV_LORA_RANK = 512
ROPE_BASE = 10000.0
ATTN_BLOCK = 128
MLA_WIDTH = MLA_HEADS * V_HEAD_DIM
GMLP_HEADS = 16
GMLP_CHUNK = 128
GMLP_WIDTH = D_MIX - MLA_WIDTH
GMLP_HEAD_DIM = GMLP_WIDTH // GMLP_HEADS
IN_WIDTH = Q_LORA_RANK + KV_LORA_RANK + QK_ROPE_DIM + 2 * GMLP_WIDTH
PEER_HEADS = 8
PEER_NKEYS = 128
PEER_N_EXPERTS = PEER_NKEYS * PEER_NKEYS
PEER_QUERY_DIM = 256
PEER_HALF = PEER_QUERY_DIM // 2
PEER_TOPK = 16
PEER_BLOCK = 128
NORM_EPS = 1e-6

kernel_name = "hybrid_mla_gmlp_peer_adaln_block"


def rms_norm(x, g):
    xf = x.astype(jnp.float32)
    y = xf * lax.rsqrt(jnp.mean(xf * xf, axis=-1, keepdims=True) + NORM_EPS)
    return (y * g.astype(jnp.float32)).astype(x.dtype)


def layer_norm(x, g):
    xf = x.astype(jnp.float32)
    mu = jnp.mean(xf, axis=-1, keepdims=True)
    var = jnp.mean(jnp.square(xf - mu), axis=-1, keepdims=True)
    return ((xf - mu) * lax.rsqrt(var + NORM_EPS) * g.astype(jnp.float32)).astype(x.dtype)


def modulate(h, shift, scale):
    return h * (1.0 + scale[:, None, :]) + shift[:, None, :]


def rope(x, cos, sin):
    x1, x2 = jnp.split(x, 2, axis=-1)
    return jnp.concatenate([x1 * cos - x2 * sin, x2 * cos + x1 * sin], axis=-1)


def mla_attention(q_lat, kv_lat, k_pe_raw, g_q, w_uq, g_kv, w_ukv, cos, sin):
    B, S, _ = q_lat.shape
    q = (rms_norm(q_lat, g_q) @ w_uq).reshape(B, S, MLA_HEADS, QK_HEAD_DIM)
    q_nope = q[..., :QK_NOPE_DIM]
    q_pe = rope(q[..., QK_NOPE_DIM:], cos[:, :, None, :], sin[:, :, None, :])
    kv = (rms_norm(kv_lat, g_kv) @ w_ukv).reshape(B, S, MLA_HEADS, QK_NOPE_DIM + V_HEAD_DIM)
    k_nope = kv[..., :QK_NOPE_DIM]
    v = kv[..., QK_NOPE_DIM:]
    k_pe = rope(k_pe_raw, cos, sin)
    nb = S // ATTN_BLOCK
    scale = 1.0 / math.sqrt(QK_HEAD_DIM)
    k_idx = jnp.arange(S)

    def to_blocks(t):
        return jnp.moveaxis(t.reshape(B, nb, ATTN_BLOCK, *t.shape[2:]), 1, 0)

    def block(args):
        qn, qp, i = args
        s = (jnp.einsum('bqhd,bkhd->bhqk', qn, k_nope)
             + jnp.einsum('bqhd,bkd->bhqk', qp, k_pe)).astype(jnp.float32) * scale
        q_idx = i * ATTN_BLOCK + jnp.arange(ATTN_BLOCK)
        causal = q_idx[:, None] >= k_idx[None, :]
        p = jax.nn.softmax(jnp.where(causal, s, -jnp.inf), axis=-1)
        return jnp.einsum('bhqk,bkhd->bqhd', p.astype(v.dtype), v)

    o = lax.map(block, (to_blocks(q_nope), to_blocks(q_pe), jnp.arange(nb)))
    return jnp.moveaxis(o, 0, 1).reshape(B, S, MLA_WIDTH)


def gmlp_sgu(z, g_sgu, w_sgu, b_sgu):
    B, S, _ = z.shape
    u, v = jnp.split(z, 2, axis=-1)
    v = layer_norm(v, g_sgu).reshape(B, S // GMLP_CHUNK, GMLP_CHUNK, GMLP_HEADS, GMLP_HEAD_DIM)
    causal = jnp.tril(jnp.ones((GMLP_CHUNK, GMLP_CHUNK), dtype=w_sgu.dtype))
    w = w_sgu * causal[None]
    sv = jnp.einsum('hts,bnshd->bnthd', w, v) + jnp.transpose(b_sgu)[None, None, :, :, None]
    return u * sv.reshape(B, S, GMLP_WIDTH)


def peer(xf, w_pq, keys, expert_u, expert_v):
    T, D = xf.shape
    blocks = xf.reshape(T // PEER_BLOCK, PEER_BLOCK, D)

    def block(xb):
        q = (xb @ w_pq).reshape(PEER_BLOCK, PEER_HEADS, 2, PEER_HALF)
        s = jnp.einsum('thpk,hpnk->thpn', q, keys).astype(jnp.float32)
        sv, si = lax.top_k(s, PEER_TOPK)
        cand = (sv[:, :, 0, :, None] + sv[:, :, 1, None, :]).reshape(PEER_BLOCK, PEER_HEADS, PEER_TOPK * PEER_TOPK)
        cidx = (si[:, :, 0, :, None] * PEER_NKEYS + si[:, :, 1, None, :]).reshape(PEER_BLOCK, PEER_HEADS, PEER_TOPK * PEER_TOPK)
        top_s, top_p = lax.top_k(cand, PEER_TOPK)
        eidx = jnp.take_along_axis(cidx, top_p, axis=-1)
        g = jax.nn.softmax(top_s, axis=-1).astype(xb.dtype)
        u = expert_u[eidx]
        a = jax.nn.gelu(jnp.einsum('thkd,td->thk', u, xb), approximate=False)
        vv = expert_v[eidx]
        return jnp.einsum('thk,thkd->td', g * a, vv)

    return lax.map(block, blocks).reshape(T, D)


def setup_inputs(seed: int = 0) -> dict:
    key = jax.random.key(seed)
    ks = jax.random.split(key, 32)
    f32 = jnp.float32
    nrm = lambda k, shape, s: jax.random.normal(k, shape, f32) * s
    gain = lambda k, shape: 1.0 + 0.1 * jax.random.normal(k, shape, f32)
    L = DEPTH
    x = jax.random.normal(ks[0], (BATCH, SEQ, D_MODEL), f32)
    c = jax.random.normal(ks[1], (BATCH, D_MODEL), f32)
    offsets = jax.random.randint(ks[2], (BATCH, 1), 0, 1024, dtype=jnp.int32)
    positions = (offsets + jnp.arange(SEQ, dtype=jnp.int32)[None, :]).astype(jnp.int32)
    return {
        "x": x,
        "c": c,
        "positions": positions,
        "w_ada": nrm(ks[3], (L, D_MODEL, 6 * D_MODEL), 0.5 * D_MODEL ** -0.5),
        "b_ada": nrm(ks[4], (L, 6 * D_MODEL), 0.02),
        "g_norm_mix": gain(ks[5], (L, D_MODEL)),
        "w_in": nrm(ks[6], (L, D_MODEL, IN_WIDTH), D_MODEL ** -0.5),
        "g_q": gain(ks[7], (L, Q_LORA_RANK)),
        "w_uq": nrm(ks[8], (L, Q_LORA_RANK, MLA_HEADS * QK_HEAD_DIM), Q_LORA_RANK ** -0.5),
        "g_kv": gain(ks[9], (L, KV_LORA_RANK)),
        "w_ukv": nrm(ks[10], (L, KV_LORA_RANK, MLA_HEADS * (QK_NOPE_DIM + V_HEAD_DIM)), KV_LORA_RANK ** -0.5),
        "g_sgu": gain(ks[11], (L, GMLP_WIDTH)),
        "w_sgu": nrm(ks[12], (L, GMLP_HEADS, GMLP_CHUNK, GMLP_CHUNK), GMLP_CHUNK ** -0.5),
        "b_sgu": gain(ks[13], (L, GMLP_HEADS, GMLP_CHUNK)),
        "beta_mla": gain(ks[14], (L, MLA_WIDTH)),
        "beta_gmlp": gain(ks[15], (L, GMLP_WIDTH)),
        "w_out": nrm(ks[16], (L, D_MIX, D_MODEL), D_MIX ** -0.5),
        "g_norm_ffn": gain(ks[17], (L, D_MODEL)),
        "w_pq": nrm(ks[18], (L, D_MODEL, PEER_HEADS * PEER_QUERY_DIM), D_MODEL ** -0.5),
        "peer_keys": nrm(ks[19], (L, PEER_HEADS, 2, PEER_NKEYS, PEER_HALF), PEER_HALF ** -0.5),
        "expert_u": nrm(ks[20], (L, PEER_N_EXPERTS, D_MODEL), D_MODEL ** -0.5),
        "expert_v": nrm(ks[21], (L, PEER_N_EXPERTS, D_MODEL), PEER_TOPK ** -0.5),
        "w_ada_f": nrm(ks[22], (D_MODEL, 2 * D_MODEL), 0.5 * D_MODEL ** -0.5),
        "b_ada_f": nrm(ks[23], (2 * D_MODEL,), 0.02),
        "g_norm_f": gain(ks[24], (D_MODEL,)),
    }


def reference(x, c, positions, w_ada, b_ada, g_norm_mix, w_in, g_q, w_uq, g_kv, w_ukv,
              g_sgu, w_sgu, b_sgu, beta_mla, beta_gmlp, w_out, g_norm_ffn, w_pq, peer_keys,
              expert_u, expert_v, w_ada_f, b_ada_f, g_norm_f):
    B, S, D = x.shape
    inv_freq = ROPE_BASE ** (-jnp.arange(0, QK_ROPE_DIM, 2, dtype=jnp.float32) / QK_ROPE_DIM)
    ang = positions.astype(jnp.float32)[..., None] * inv_freq
    cos = jnp.cos(ang).astype(x.dtype)
    sin = jnp.sin(ang).astype(x.dtype)
    c_act = jax.nn.silu(c)
    splits = [Q_LORA_RANK, Q_LORA_RANK + KV_LORA_RANK, Q_LORA_RANK + KV_LORA_RANK + QK_ROPE_DIM]
    for l in range(DEPTH):
        mod = c_act @ w_ada[l] + b_ada[l]
        sh_a, sc_a, gt_a, sh_f, sc_f, gt_f = jnp.split(mod, 6, axis=-1)
        h = modulate(rms_norm(x, g_norm_mix[l]), sh_a, sc_a)
        z = h @ w_in[l]
        q_lat, kv_lat, k_pe_raw, z_g = jnp.split(z, splits, axis=-1)
        y_mla = mla_attention(q_lat, kv_lat, k_pe_raw, g_q[l], w_uq[l], g_kv[l], w_ukv[l], cos, sin)
        y_g = gmlp_sgu(jax.nn.gelu(z_g, approximate=False), g_sgu[l], w_sgu[l], b_sgu[l])
        y = jnp.concatenate([rms_norm(y_mla, beta_mla[l]), rms_norm(y_g, beta_gmlp[l])], axis=-1)
        x = x + gt_a[:, None, :] * (y @ w_out[l])
        h = modulate(rms_norm(x, g_norm_ffn[l]), sh_f, sc_f)
        f = peer(h.reshape(B * S, D), w_pq[l], peer_keys[l], expert_u[l], expert_v[l]).reshape(B, S, D)
        x = x + gt_f[:, None, :] * f
    sh, sc = jnp.split(c_act @ w_ada_f + b_ada_f, 2, axis=-1)
    return modulate(rms_norm(x, g_norm_f), sh, sc)
```

```python
from contextlib import ExitStack
import math
import numpy as np
import concourse.bass as bass
import concourse.mybir as mybir
from concourse.bass_utils import run_bass_kernel_spmd

F32 = mybir.dt.float32
BF16 = mybir.dt.bfloat16
I32 = mybir.dt.int32
ALU = mybir.AluOpType
AF = mybir.ActivationFunctionType
AX = mybir.AxisListType

D = 4096
S = 4096
NT = 8
EPS = 1e-6
TWO_PI = 2.0 * math.pi
C1 = 6.28125
C2 = TWO_PI - C1


class Tile:
    __slots__ = ("t", "w", "r", "sem", "dcnt", "name")

    def __init__(self, t, name):
        self.t = t
        self.name = name
        self.w = None
        self.r = {}
        self.sem = None
        self.dcnt = 0

    def __getitem__(self, k):
        return self.t[k]


class KB:
    ENG = ("pe", "dve", "act", "pool", "sp")

    def __init__(self, nc, n_dma_sems=80):
        self.nc = nc
        self.es = ExitStack()
        self.eng = {"pe": nc.tensor, "dve": nc.vector, "act": nc.scalar,
                    "pool": nc.gpsimd, "sp": nc.sync}
        self.sem = {}
        self.cnt = {}
        for e in self.ENG:
            self.sem[e] = self.es.enter_context(nc.semaphore("s_" + e))
            self.cnt[e] = 0
        self.seen = {e: {} for e in self.ENG}
        self.dsem = [self.es.enter_context(nc.semaphore("d%d" % i)) for i in range(n_dma_sems)]
        self.dval = [0] * n_dma_sems
        self.dfree = list(range(n_dma_sems))
        self.stacks = []
        self.tiles = []
        self.uid = 0
        self.ninst = 0
        self.nwaits = 0

    def begin(self):
        self.stacks.append(ExitStack())
        self.tiles.append([])

    def end(self):
        self.barrier()
        for t in self.tiles.pop():
            if t.sem is not None:
                self.dfree.append(t.sem)
                t.sem = None
        self.stacks.pop().close()

    def sb(self, shape, dtype, name):
        self.uid += 1
        st = self.stacks[-1] if self.stacks else self.es
        t = st.enter_context(self.nc.sbuf_tensor("%s_%d" % (name, self.uid), list(shape), dtype))
        tl = Tile(t, name)
        if self.stacks:
            self.tiles[-1].append(tl)
        return tl

    def ps(self, shape, dtype, name):
        self.uid += 1
        st = self.stacks[-1] if self.stacks else self.es
        esz = 2 if dtype == BF16 else 4
        assert len(shape) == 2
        per_bank = 2048 // esz
        ncol = ((shape[1] + per_bank - 1) // per_bank) * per_bank
        t = st.enter_context(self.nc.psum_tensor("%s_%d" % (name, self.uid), [shape[0], ncol], dtype))
        return Tile(t[:, 0:shape[1]], name)

    def alias(self, tile, name):
        return Tile(tile.t, name)

    def _semof(self, key):
        return self.sem[key] if isinstance(key, str) else self.dsem[key]

    def _need(self, e, ev, needs):
        if ev is None:
            return
        k, v = ev
        if self.seen[e].get(k, 0) >= v:
            return
        if needs.get(k, 0) < v:
            needs[k] = v

    def _emit_waits(self, e, needs):
        for k, v in needs.items():
            self.eng[e].wait_ge(self._semof(k), v)
            self.seen[e][k] = v
            self.nwaits += 1

    def _deps(self, e, r, w, needs):
        for t in r:
            self._need(e, t.w, needs)
        for t in w:
            if not (e == "pe" and t.w is not None and t.w[0] == "pe"):
                self._need(e, t.w, needs)
            for k, v in t.r.items():
                self._need(e, (k, v), needs)

    def op(self, e, fn, r=(), w=()):
        needs = {}
        self._deps(e, r, w, needs)
        self._emit_waits(e, needs)
        inst = fn()
        self.cnt[e] += 1
        inst.then_inc(self.sem[e], 1)
        ev = (e, self.cnt[e])
        for t in w:
            t.w = ev
            t.r = {}
        for t in r:
            if t not in w:
                t.r[e] = self.cnt[e]
        self.ninst += 1
        return inst

    def dma(self, q, out, in_, tile, store=False, **kw):
        needs = {}
        if store:
            self._deps(q, [tile], [], needs)
        else:
            self._deps(q, [], [tile], needs)
        self._emit_waits(q, needs)
        if tile.sem is None:
            tile.sem = self.dfree.pop()
        inst = self.eng[q].dma_start(out=out, in_=in_, **kw)
        inst.then_inc(self.dsem[tile.sem], 16)
        self.dval[tile.sem] += 16
        tile.dcnt = self.dval[tile.sem]
        if store:
            tile.r[tile.sem] = tile.dcnt
        else:
            tile.w = (tile.sem, tile.dcnt)
            tile.r = {}
        self.ninst += 1
        return inst

    def barrier(self):
        for e in self.ENG:
            needs = {}
            for k2 in self.ENG:
                if k2 != e and self.cnt[k2] > 0:
                    self._need(e, (k2, self.cnt[k2]), needs)
            for i, v in enumerate(self.dval):
                if v > 0:
                    self._need(e, (i, v), needs)
            self._emit_waits(e, needs)

    def finish(self):
        while self.stacks:
            self.end()
        self.barrier()
        self.es.close()


class Prog:
    def __init__(self, stop_after=None, debug=False):
        self.stop_after = stop_after
        self.debug = debug
        nc = bass.Bass("TRN2", target_bir_lowering=False)
        self.nc = nc
        self.k = KB(nc)
        self.din = {}
        self.rr = 0

    def inp(self, name, shape, dtype=F32):
        ap = self.nc.dram_tensor(name, list(shape), dtype, kind="ExternalInput").ap()
        self.din[name] = ap
        return ap

    def scratch(self, name, shape, dtype):
        return self.nc.dram_tensor(name, list(shape), dtype,
                                   kind=("ExternalOutput" if self.debug else "Internal")).ap()

    def declare(self):
        i = self.inp
        self.xq = i("xq", [1024, D])
        self.xall = i("xall", [S, D])
        self.posq = i("posq", [128, NT], I32)
        self.posall = i("posall", [128, 32], I32)
        self.qidx = i("qidx", [1, 1024])
        self.cvec = i("cvec", [128, 32])
        self.invf = i("invf", [1, 32])
        self.piota = i("piota", [128, 1])
        self.ident = i("ident", [128, 128])
        self.tril = i("tril", [128, 128])
        self.wada = i("wada", [128, 128, 32, 256])
        self.bada = i("bada", [128, 256])
        self.gmix = i("gmix", [128, 32])
        self.gffn = i("gffn", [128, 32])
        self.gfin = i("gfin", [1, D])
        self.wkv = i("wkv", [128, 32, 576])
        self.wqg = i("wqg", [10, 128, 32, 512])
        self.gq = i("gq", [128, 8])
        self.gkv = i("gkv", [128, 4])
        self.wuq = i("wuq", [128, 8, 3072])
        self.wuk = i("wuk", [128, 4, 2048])
        self.wuv = i("wuv", [128, 4, 2048])
        self.gsgu = i("gsgu", [1, 2048])
        self.wsgu = i("wsgu", [128, 16, 128])
        self.bsgu = i("bsgu", [128, 16])
        self.bmla = i("bmla", [128, 16])
        self.bgm = i("bgm", [128, 16])
        self.wout = i("wout", [8, 128, 32, 512])
        self.wpq = i("wpq", [8, 128, 32, 256])
        self.keysT = i("keysT", [128, 16, 128])
        self.U = i("U", [128, 128, 32, 128])
        self.V = i("V", [16, 8, 128, 8, 512])
        self.out = self.nc.dram_tensor("out", [1024, D], F32, kind="ExternalOutput").ap()
        s = self.scratch
        self.modrows = s("modrows", [256, 128], F32)
        self.kT_d = s("kT_d", [16, 128, S], BF16)
        self.v_d = s("v_d", [S, 2048], BF16)
        self.zq_d = s("zq_d", [1024, 1024], F32)
        self.zg_d = s("zg_d", [1024, 4096], F32)
        self.qT_d = s("qT_d", [16, 128, 1024], BF16)
        self.qpeT_d = s("qpeT_d", [16, 64, 1024], BF16)
        self.yT_d = s("yT_d", [16, 128, 1024], BF16)
        self.x1_d = s("x1_d", [1024, D], F32)

    def evac_engine(self):
        self.rr += 1
        return "dve" if (self.rr % 2) else "act"

    def rstd_from_ssq(self, ssq, rstd, n):
        k, nc = self.k, self.nc
        k.op("act", lambda: nc.scalar.activation(out=rstd[:, :], in_=ssq[:, :], func=AF.Sqrt,
                                                 scale=1.0 / n, bias=self.epst[:, 0:1]),
             r=[ssq, self.epst], w=[rstd])
        k.op("dve", lambda: nc.vector.reciprocal(out=rstd[:, :], in_=rstd[:, :]), r=[rstd], w=[rstd])

    def sumsq(self, src_ap, src_tiles, junk_ap, junk_tile, ssq):
        k, nc = self.k, self.nc
        k.op("pool", lambda: nc.gpsimd.memset(ssq[:, :], 0.0), w=[ssq])
        k.op("act", lambda: nc.scalar.activation(out=junk_ap, in_=src_ap, func=AF.Square,
                                                 accum_out=ssq[:, :]),
             r=src_tiles, w=[junk_tile, ssq])

    def transpose_evac(self, src_tile, src_ap, np_out, dst_tile, dst_ap, scale_ap=None, bias_ap=None,
                       extra_r=()):
        k, nc = self.k, self.nc
        slot = self.tp[self.tpi % len(self.tp)]
        self.tpi += 1
        k.op("pe", lambda: nc.tensor.transpose(out=slot[0:np_out, self.tpcol(slot)], in_=src_ap,
                                               identity=self.identb[:, :]),
             r=[src_tile, self.identb], w=[slot])
        e = self.evac_engine()
        pin = slot[0:np_out, self.tpcol(slot)]
        if scale_ap is None:
            if e == "dve":
                k.op("dve", lambda: nc.vector.tensor_copy(out=dst_ap, in_=pin), r=[slot], w=[dst_tile])
            else:
                k.op("act", lambda: nc.scalar.copy(out=dst_ap, in_=pin), r=[slot], w=[dst_tile])
        elif bias_ap is None:
            if e == "dve":
                k.op("dve", lambda: nc.vector.tensor_scalar(out=dst_ap, in0=pin, scalar1=scale_ap, scalar2=None,
                                                            op0=ALU.mult), r=[slot] + list(extra_r), w=[dst_tile])
            else:
                k.op("act", lambda: nc.scalar.activation(out=dst_ap, in_=pin, func=AF.Copy, scale=scale_ap),
                     r=[slot] + list(extra_r), w=[dst_tile])
        else:
            if e == "dve":
                k.op("dve", lambda: nc.vector.tensor_scalar(out=dst_ap, in0=pin, scalar1=scale_ap, scalar2=bias_ap,
                                                            op0=ALU.mult, op1=ALU.add),
                     r=[slot] + list(extra_r), w=[dst_tile])
            else:
                k.op("act", lambda: nc.scalar.activation(out=dst_ap, in_=pin, func=AF.Identity, scale=scale_ap,
                                                         bias=bias_ap),
                     r=[slot] + list(extra_r), w=[dst_tile])

    def tpcol(self, slot):
        return slice(0, 128)

    def make_tp(self, n=2):
        self.tp = [self.k.ps([128, 1024], BF16, "tpb%d" % i) for i in range(n)]
        self.tpi = 0

    def sincos(self, ang, shape, sin_t, cos_t, tmp_f, tmp_i):
        k, nc = self.k, self.nc

        def full(t):
            return t[tuple(slice(None) for _ in shape)]

        for (dst, shift) in ((sin_t, 0.0), (cos_t, math.pi / 2)):
            k.op("dve", lambda: nc.vector.tensor_scalar(out=full(tmp_f), in0=full(ang), scalar1=shift,
                                                        scalar2=1.0 / TWO_PI, op0=ALU.add, op1=ALU.mult),
                 r=[ang], w=[tmp_f])
            k.op("dve", lambda: nc.vector.tensor_copy(out=full(tmp_i), in_=full(tmp_f)), r=[tmp_f], w=[tmp_i])
            k.op("dve", lambda: nc.vector.tensor_copy(out=full(tmp_f), in_=full(tmp_i)), r=[tmp_i], w=[tmp_f])
            k.op("dve", lambda: nc.vector.scalar_tensor_tensor(out=full(dst), in0=full(tmp_f), scalar=-C1,
                                                               in1=full(ang), op0=ALU.mult, op1=ALU.add),
                 r=[tmp_f, ang], w=[dst])
            k.op("dve", lambda: nc.vector.scalar_tensor_tensor(out=full(dst), in0=full(tmp_f), scalar=-C2,
                                                               in1=full(dst), op0=ALU.mult, op1=ALU.add),
                 r=[tmp_f, dst], w=[dst])
            if shift != 0.0:
                k.op("dve", lambda: nc.vector.tensor_scalar(out=full(dst), in0=full(dst), scalar1=shift,
                                                            scalar2=None, op0=ALU.add), r=[dst], w=[dst])
            k.op("dve", lambda: nc.vector.tensor_scalar(out=full(dst), in0=full(dst), scalar1=math.pi,
                                                        scalar2=-math.pi, op0=ALU.min, op1=ALU.max),
                 r=[dst], w=[dst])
            k.op("act", lambda: nc.scalar.activation(out=full(dst), in_=full(dst), func=AF.Sin), r=[dst], w=[dst])

    def rope_tables(self, pos_dram, nblk, name):
        k, nc = self.k, self.nc
        posi = k.sb([128, nblk], I32, name + "posi")
        posf = k.sb([128, nblk], F32, name + "posf")
        invb = k.sb([128, 32], F32, name + "invb")
        ang = k.sb([128, nblk, 32], F32, name + "ang")
        tf = k.sb([128, nblk, 32], F32, name + "tf")
        ti = k.sb([128, nblk, 32], I32, name + "ti")
        sin_t = k.sb([128, nblk, 32], F32, name + "sin")
        cos_t = k.sb([128, nblk, 32], F32, name + "cos")
        k.dma("sp", posi[:, :], pos_dram, posi)
        k.dma("sp", invb[:, :], self.invf.to_broadcast([128, 32]), invb)
        k.op("dve", lambda: nc.vector.tensor_copy(out=posf[:, :], in_=posi[:, :]), r=[posi], w=[posf])
        k.op("dve", lambda: nc.vector.tensor_tensor(
            out=ang[:, :, :], in0=posf[:, :].unsqueeze(2).to_broadcast([128, nblk, 32]),
            in1=invb[:, :].unsqueeze(1).to_broadcast([128, nblk, 32]), op=ALU.mult), r=[posf, invb], w=[ang])
        self.sincos(ang, [128, nblk, 32], sin_t, cos_t, tf, ti)
        return sin_t, cos_t

    def rope_apply(self, x1, x2, cosb, sinb, o1, o2, shape, srcs, dst, tmps):
        k, nc = self.k, self.nc
        ta, tb = tmps

        def full(t):
            return t[tuple(slice(None) for _ in shape)]

        k.op("dve", lambda: nc.vector.tensor_tensor(out=full(ta), in0=x1, in1=cosb, op=ALU.mult), r=srcs, w=[ta])
        k.op("pool", lambda: nc.gpsimd.tensor_tensor(out=full(tb), in0=x2, in1=sinb, op=ALU.mult), r=srcs, w=[tb])
        k.op("dve", lambda: nc.vector.tensor_tensor(out=o1, in0=full(ta), in1=full(tb), op=ALU.subtract),
             r=[ta, tb], w=[dst])
        k.op("dve", lambda: nc.vector.tensor_tensor(out=full(ta), in0=x2, in1=cosb, op=ALU.mult), r=srcs, w=[ta])
        k.op("pool", lambda: nc.gpsimd.tensor_tensor(out=full(tb), in0=x1, in1=sinb, op=ALU.mult), r=srcs, w=[tb])
        k.op("dve", lambda: nc.vector.tensor_tensor(out=o2, in0=full(ta), in1=full(tb), op=ALU.add),
             r=[ta, tb], w=[dst])

    def norm_featmajor(self, xt, F, A, B, dstT, col0, junk, xn, ssq, rstd, boff=0):
        k, nc = self.k, self.nc
        self.nrm = getattr(self, "nrm", 0) + 1

        def pick(x):
            return x[self.nrm % len(x)] if isinstance(x, list) else x

        junk, xn, ssq, rstd = pick(junk), pick(xn), pick(ssq), pick(rstd)
        self.sumsq(xt[:, 0:F], [xt], junk[:, 0:F], junk, ssq)
        self.rstd_from_ssq(ssq, rstd, F)
        k.op("act", lambda: nc.scalar.activation(out=xn[:, 0:F], in_=xt[:, 0:F], func=AF.Copy, scale=rstd[:, 0:1]),
             r=[xt, rstd], w=[xn])
        for kc in range(F // 128):
            self.transpose_evac(xn, xn[:, kc * 128:(kc + 1) * 128], 128, dstT,
                                dstT[:, kc, col0:col0 + 128],
                                scale_ap=A[:, kc:kc + 1],
                                bias_ap=(B[:, boff + kc:boff + kc + 1] if B is not None else None),
                                extra_r=[A] + ([B] if B is not None else []))

    def phase_setup(self):
        k, nc = self.k, self.nc
        self.epst = k.sb([128, 1], F32, "epst")
        k.op("pool", lambda: nc.gpsimd.memset(self.epst[:, :], EPS), w=[self.epst])
        self.identf = k.sb([128, 128], F32, "identf")
        self.identb = k.sb([128, 128], BF16, "identb")
        k.dma("sp", self.identf[:, :], self.ident, self.identf)
        k.op("dve", lambda: nc.vector.tensor_copy(out=self.identb[:, :], in_=self.identf[:, :]),
             r=[self.identf], w=[self.identb])
        self.modT = k.sb([128, 256], F32, "modT")
        self.Amix = k.sb([128, 32], F32, "Amix")
        self.Affn = k.sb([128, 32], F32, "Affn")

    def phase_A(self):
        k, nc = self.k, self.nc
        k.begin()
        cT = k.sb([128, 32], F32, "cT")
        cact = k.sb([128, 32], F32, "cact")
        bT = k.sb([128, 256], F32, "bT")
        g1 = k.sb([128, 32], F32, "g1")
        g2 = k.sb([128, 32], F32, "g2")
        modps = k.ps([128, 256], F32, "modps")
        tpf = k.ps([128, 128], F32, "tpf")
        rows = k.sb([128, 128], F32, "rows")
        wts = [k.sb([128, 32, 256], BF16, "wadat%d" % i) for i in range(4)]
        cact_b = k.sb([128, 32], BF16, "cact_b")
        k.dma("sp", cT[:, :], self.cvec, cT)
        k.dma("sp", bT[:, :], self.bada, bT)
        k.dma("sp", g1[:, :], self.gmix, g1)
        k.dma("sp", g2[:, :], self.gffn, g2)
        k.op("act", lambda: nc.scalar.activation(out=cact[:, :], in_=cT[:, :], func=AF.Silu), r=[cT], w=[cact])
        k.op("dve", lambda: nc.vector.tensor_copy(out=cact_b[:, :], in_=cact[:, :]), r=[cact], w=[cact_b])
        for blk in range(128):
            wt = wts[blk % 4]
            k.dma("pool", wt[:, :, :], self.wada[blk], wt)
            for n2 in range(2):
                j = blk * 2 + n2
                for kc in range(32):
                    k.op("pe", lambda: nc.tensor.matmul(modps[:, j:j + 1], lhsT=wt[:, kc, n2 * 128:(n2 + 1) * 128],
                                                        rhs=cact_b[:, kc:kc + 1], start=(kc == 0), stop=(kc == 31)),
                         r=[wt, cact_b], w=[modps])
        modT = self.modT
        k.op("dve", lambda: nc.vector.tensor_tensor(out=modT[:, :], in0=modps[:, :], in1=bT[:, :], op=ALU.add),
             r=[modps, bT], w=[modT])
        for (A, g, c0) in ((self.Amix, g1, 32), (self.Affn, g2, 128)):
            k.op("dve", lambda: nc.vector.scalar_tensor_tensor(out=A[:, :], in0=modT[:, c0:c0 + 32], scalar=1.0,
                                                               in1=g[:, :], op0=ALU.add, op1=ALU.mult),
                 r=[modT, g], w=[A])
        for half in range(2):
            k.op("pe", lambda: nc.tensor.transpose(out=tpf[:, :], in_=modT[:, half * 128:(half + 1) * 128],
                                                   identity=self.identf[:, :]), r=[modT, self.identf], w=[tpf])
            k.op("dve", lambda: nc.vector.tensor_copy(out=rows[:, :], in_=tpf[:, :]), r=[tpf], w=[rows])
            k.dma("sp", self.modrows[half * 128:(half + 1) * 128, :], rows[:, :], rows, store=True)
        k.end()

    def mod_bc(self, tile, seg):
        flat = self.modrows.rearrange("a b -> (a b)")
        src = flat[seg * 4096:(seg + 1) * 4096]
        self.k.dma("sp", tile[:, :], src.partition_broadcast(128), tile)

    def phase_B(self):
        k, nc = self.k, self.nc
        k.begin()
        self.make_tp()
        wkv = k.sb([128, 32, 576], BF16, "wkv")
        wuk = k.sb([128, 4, 2048], BF16, "wuk")
        wuv = k.sb([128, 4, 2048], BF16, "wuv")
        gkv = k.sb([128, 4], F32, "gkv")
        for q4 in range(4):
            k.dma("pool", wkv[:, q4 * 8:(q4 + 1) * 8, :], self.wkv[:, q4 * 8:(q4 + 1) * 8, :], wkv)
        k.dma("pool", wuk[:, :, :], self.wuk, wuk)
        k.dma("pool", wuv[:, :, :], self.wuv, wuv)
        k.dma("sp", gkv[:, :], self.gkv, gkv)
        sin_t, cos_t = self.rope_tables(self.posall, 32, "B")
        xts = [k.sb([128, D], F32, "xt%d" % i) for i in range(2)]
        junk = k.sb([128, D], BF16, "junk")
        xn = [k.sb([128, D], BF16, "xn%d" % i) for i in range(2)]
        ssq = [k.sb([128, 1], F32, "ssq%d" % i) for i in range(2)]
        rstd = [k.sb([128, 1], F32, "rstd%d" % i) for i in range(2)]
        ssq2s = [k.sb([128, 1], F32, "ssq2%d" % i) for i in range(2)]
        rstd2s = [k.sb([128, 1], F32, "rstd2%d" % i) for i in range(2)]
        hTs = [k.sb([128, 32, 128], BF16, "hT%d" % i) for i in range(2)]
        kvns = [k.sb([128, 512], BF16, "kvn%d" % i) for i in range(2)]
        pe_sbs = [k.sb([128, 64], F32, "pe_sb%d" % i) for i in range(2)]
        kpes = [k.sb([128, 64], BF16, "kpe%d" % i) for i in range(2)]
        tas = [k.sb([128, 32], F32, "ta%d" % i) for i in range(2)]
        tbs = [k.sb([128, 32], F32, "tb%d" % i) for i in range(2)]
        kvnTs = [k.sb([128, 4, 512], BF16, "kvnT%d" % i) for i in range(2)]
        ksts = [k.sb([128, 512], BF16, "kst%d" % i) for i in range(3)]
        vsts = [k.sb([128, 2048], BF16, "vst%d" % i) for i in range(2)]
        ps_kv = k.ps([128, 512], F32, "ps_kv")
        ps_pe = k.ps([128, 64], F32, "ps_pe")
        ps_k = [k.ps([128, 512], F32, "ps_k%d" % i) for i in range(2)]
        ps_v = [k.ps([128, 512], F32, "ps_v%d" % i) for i in range(2)]
        nk = 0
        nv = 0
        bstage = 9
        for kb in range(32):
            g = kb // 4
            xt = xts[kb % 2]
            hT = hTs[kb % 2]
            kvnT = kvnTs[g % 2]
            ssq2, rstd2, kvn, pe_sb, kpe = ssq2s[kb % 2], rstd2s[kb % 2], kvns[kb % 2], pe_sbs[kb % 2], kpes[kb % 2]
            ta, tb = tas[kb % 2], tbs[kb % 2]
            k.dma("sp", xt[:, :], self.xall[kb * 128:(kb + 1) * 128, :], xt)
            if bstage < 1:
                continue
            self.norm_featmajor(xt, D, self.Amix, self.modT, hT, 0, junk, xn, ssq, rstd)
            if bstage < 2:
                continue
            for kc in range(32):
                k.op("pe", lambda: nc.tensor.matmul(ps_kv[:, :], lhsT=hT[:, kc, :], rhs=wkv[:, kc, 0:512],
                                                    start=(kc == 0), stop=(kc == 31)), r=[hT, wkv], w=[ps_kv])
                k.op("pe", lambda: nc.tensor.matmul(ps_pe[:, :], lhsT=hT[:, kc, :], rhs=wkv[:, kc, 512:576],
                                                    start=(kc == 0), stop=(kc == 31)), r=[hT, wkv], w=[ps_pe])
            if bstage < 3:
                continue
            self.sumsq(ps_kv[:, :], [ps_kv], junk[:, 0:512], junk, ssq2)
            self.rstd_from_ssq(ssq2, rstd2, 512)
            k.op("act", lambda: nc.scalar.activation(out=kvn[:, :], in_=ps_kv[:, :], func=AF.Copy,
                                                     scale=rstd2[:, 0:1]), r=[ps_kv, rstd2], w=[kvn])
            k.op("dve", lambda: nc.vector.tensor_copy(out=pe_sb[:, :], in_=ps_pe[:, :]), r=[ps_pe], w=[pe_sb])
            self.rope_apply(pe_sb[:, 0:32], pe_sb[:, 32:64], cos_t[:, kb, :], sin_t[:, kb, :],
                            kpe[:, 0:32], kpe[:, 32:64], [128, 32], [pe_sb, cos_t, sin_t], kpe, (ta, tb))
            if bstage < 4:
                continue
            for c in range(4):
                self.transpose_evac(kvn, kvn[:, c * 128:(c + 1) * 128], 128, kvnT,
                                    kvnT[:, c, (kb % 4) * 128:(kb % 4 + 1) * 128],
                                    scale_ap=gkv[:, c:c + 1], extra_r=[gkv])
            self.transpose_evac(kpe, kpe[:, :], 64, self.kpeT, self.kpeT[0:64, kb * 128:(kb + 1) * 128])
            if bstage < 5:
                continue
            if kb % 4 == 3:
                for h in range(16):
                    pk = ps_k[nk % 2]
                    kst = ksts[nk % 3]
                    nk += 1
                    for c in range(4):
                        k.op("pe", lambda: nc.tensor.matmul(pk[:, :], lhsT=wuk[:, c, h * 128:(h + 1) * 128],
                                                            rhs=kvnT[:, c, :], start=(c == 0), stop=(c == 3)),
                             r=[wuk, kvnT], w=[pk])
                    e = self.evac_engine()
                    if e == "dve":
                        k.op("dve", lambda: nc.vector.tensor_copy(out=kst[:, :], in_=pk[:, :]), r=[pk], w=[kst])
                    else:
                        k.op("act", lambda: nc.scalar.copy(out=kst[:, :], in_=pk[:, :]), r=[pk], w=[kst])
                    k.dma("sp", self.kT_d[h, :, g * 512:(g + 1) * 512], kst[:, :], kst, store=True)
                for tb4 in range(4):
                    vst = vsts[tb4 % 2]
                    for hg in range(4):
                        pv = ps_v[nv % 2]
                        nv += 1
                        for c in range(4):
                            k.op("pe", lambda: nc.tensor.matmul(pv[:, :], lhsT=kvnT[:, c, tb4 * 128:(tb4 + 1) * 128],
                                                                rhs=wuv[:, c, hg * 512:(hg + 1) * 512],
                                                                start=(c == 0), stop=(c == 3)),
                                 r=[wuv, kvnT], w=[pv])
                        e = self.evac_engine()
                        if e == "dve":
                            k.op("dve", lambda: nc.vector.tensor_copy(out=vst[:, hg * 512:(hg + 1) * 512],
                                                                      in_=pv[:, :]), r=[pv], w=[vst])
                        else:
                            k.op("act", lambda: nc.scalar.copy(out=vst[:, hg * 512:(hg + 1) * 512], in_=pv[:, :]),
                                 r=[pv], w=[vst])
                    r0 = (g * 4 + tb4) * 128
                    k.dma("sp", self.v_d[r0:r0 + 128, :], vst[:, :], vst, store=True)
        k.end()

    def phase_C(self):
        k, nc = self.k, self.nc
        k.begin()
        self.make_tp()
        hTq = k.sb([128, 32, 1024], BF16, "hTq")
        k.begin()
        xts = [k.sb([128, D], F32, "xt%d" % i) for i in range(2)]
        junk = k.sb([128, D], BF16, "junk")
        xn = [k.sb([128, D], BF16, "xn%d" % i) for i in range(2)]
        ssq = [k.sb([128, 1], F32, "ssq%d" % i) for i in range(2)]
        rstd = [k.sb([128, 1], F32, "rstd%d" % i) for i in range(2)]
        for tt in range(NT):
            xt = xts[tt % 2]
            k.dma("sp", xt[:, :], self.xq[tt * 128:(tt + 1) * 128, :], xt)
            self.norm_featmajor(xt, D, self.Amix, self.modT, hTq, tt * 128, junk, xn, ssq, rstd)
        k.end()
        k.begin()
        wps = [k.sb([128, 32, 512], BF16, "wp%d" % i) for i in range(2)]
        pss = [k.ps([128, 512], F32, "psz%d" % i) for i in range(3)]
        sts = [k.sb([128, 512], F32, "stz%d" % i) for i in range(3)]
        n = 0
        for cb in range(10):
            wp = wps[cb % 2]
            for q4 in range(4):
                k.dma("pool", wp[:, q4 * 8:(q4 + 1) * 8, :], self.wqg[cb, :, q4 * 8:(q4 + 1) * 8, :], wp)
            for tt in range(NT):
                ps = pss[n % 3]
                st = sts[n % 3]
                n += 1
                for kc in range(32):
                    k.op("pe", lambda: nc.tensor.matmul(ps[:, :], lhsT=hTq[:, kc, tt * 128:(tt + 1) * 128],
                                                        rhs=wp[:, kc, :], start=(kc == 0), stop=(kc == 31)),
                         r=[hTq, wp], w=[ps])
                if cb < 2:
                    k.op("dve", lambda: nc.vector.tensor_copy(out=st[:, :], in_=ps[:, :]), r=[ps], w=[st])
                    k.dma("sp", self.zq_d[tt * 128:(tt + 1) * 128, cb * 512:(cb + 1) * 512], st[:, :], st, store=True)
                else:
                    k.op("act", lambda: nc.scalar.activation(out=st[:, :], in_=ps[:, :], func=AF.Gelu),
                         r=[ps], w=[st])
                    k.dma("sp", self.zg_d[tt * 128:(tt + 1) * 128, (cb - 2) * 512:(cb - 1) * 512], st[:, :], st,
                          store=True)
        k.end()
        k.end()
        if self.stop_after == "C2":
            return
        k.begin()
        self.make_tp()
        wuq = k.sb([128, 8, 3072], BF16, "wuq")
        gq = k.sb([128, 8], F32, "gq")
        for q4 in range(4):
            k.dma("pool", wuq[:, q4 * 2:(q4 + 1) * 2, :], self.wuq[:, q4 * 2:(q4 + 1) * 2, :], wuq)
        k.dma("sp", gq[:, :], self.gq, gq)
        sin_t, cos_t = self.rope_tables(self.posq, NT, "C")
        qlats = [k.sb([128, 1024], F32, "qlat%d" % i) for i in range(2)]
        junk = k.sb([128, 1024], BF16, "junkq")
        qn = k.sb([128, 1024], BF16, "qn")
        ssq = k.sb([128, 1], F32, "ssq")
        rstd = k.sb([128, 1], F32, "rstd")
        qnT = k.sb([128, 8, 128], BF16, "qnT")
        q_sb = k.sb([128, 3072], F32, "q_sb")
        qb = k.sb([128, 16, 192], BF16, "qb")
        ta = k.sb([128, 16, 32], F32, "ta")
        tb = k.sb([128, 16, 32], F32, "tb")
        qTst = [k.sb([128, 16, 128], BF16, "qTst%d" % i) for i in range(2)]
        qpeTst = [k.sb([64, 16, 128], BF16, "qpeTst%d" % i) for i in range(2)]
        psq = [k.ps([128, 512], F32, "psq%d" % i) for i in range(2)]
        n = 0
        for tt in range(NT):
            ql = qlats[tt % 2]
            k.dma("sp", ql[:, :], self.zq_d[tt * 128:(tt + 1) * 128, :], ql)
            self.norm_featmajor(ql, 1024, gq, None, qnT, 0, junk, qn, ssq, rstd)
            for cb in range(6):
                ps = psq[n % 2]
                n += 1
                for c in range(8):
                    k.op("pe", lambda: nc.tensor.matmul(ps[:, :], lhsT=qnT[:, c, :],
                                                        rhs=wuq[:, c, cb * 512:(cb + 1) * 512],
                                                        start=(c == 0), stop=(c == 7)), r=[qnT, wuq], w=[ps])
                e = self.evac_engine()
                if e == "dve":
                    k.op("dve", lambda: nc.vector.tensor_copy(out=q_sb[:, cb * 512:(cb + 1) * 512], in_=ps[:, :]),
                         r=[ps], w=[q_sb])
                else:
                    k.op("act", lambda: nc.scalar.copy(out=q_sb[:, cb * 512:(cb + 1) * 512], in_=ps[:, :]),
                         r=[ps], w=[q_sb])
            qv = q_sb[:, :].rearrange("p (h d) -> p h d", h=16)
            k.op("act", lambda: nc.scalar.copy(out=qb[:, :, 0:128], in_=qv[:, :, 0:128]), r=[q_sb], w=[qb])
            cosb = cos_t[:, tt, :].unsqueeze(1).to_broadcast([128, 16, 32])
            sinb = sin_t[:, tt, :].unsqueeze(1).to_broadcast([128, 16, 32])
            self.rope_apply(qv[:, :, 128:160], qv[:, :, 160:192], cosb, sinb,
                            qb[:, :, 128:160], qb[:, :, 160:192], [128, 16, 32], [q_sb, cos_t, sin_t], qb, (ta, tb))
            qT = qTst[tt % 2]
            qpT = qpeTst[tt % 2]
            for h in range(16):
                self.transpose_evac(qb, qb[:, h, 0:128], 128, qT, qT[:, h, :])
                self.transpose_evac(qb, qb[:, h, 128:192], 64, qpT, qpT[0:64, h, :])
            k.dma("sp", self.qT_d[:, :, tt * 128:(tt + 1) * 128].rearrange("h d t -> d h t"), qT[:, :, :], qT,
                  store=True)
            k.dma("sp", self.qpeT_d[:, :, tt * 128:(tt + 1) * 128].rearrange("h d t -> d h t"), qpT[:, :, :], qpT,
                  store=True)
        k.end()
        if self.stop_after == "C3":
            return
        k.begin()
        self.make_tp()
        gs_bc = k.sb([128, 2048], F32, "gs_bc")
        wsf = k.sb([128, 16, 128], F32, "wsf")
        trl = k.sb([128, 128], F32, "trl")
        WmT = k.sb([128, 16, 128], BF16, "WmT")
        bT = k.sb([128, 16], F32, "bT")
        bgm = k.sb([128, 16], F32, "bgm")
        k.dma("sp", gs_bc[:, :], self.gsgu.to_broadcast([128, 2048]), gs_bc)
        k.dma("sp", wsf[:, :, :], self.wsgu, wsf)
        k.dma("sp", trl[:, :], self.tril, trl)
        k.dma("sp", bT[:, :], self.bsgu, bT)
        k.dma("sp", bgm[:, :], self.bgm, bgm)
        k.op("dve", lambda: nc.vector.tensor_tensor(out=WmT[:, :, :], in0=wsf[:, :, :],
                                                    in1=trl[:, :].unsqueeze(1).to_broadcast([128, 16, 128]),
                                                    op=ALU.mult), r=[wsf, trl], w=[WmT])
        us = [k.sb([128, 2048], F32, "u%d" % i) for i in range(2)]
        vs = [k.sb([128, 2048], F32, "v%d" % i) for i in range(2)]
        vc = k.sb([128, 2048], F32, "vc")
        junk = k.sb([128, 2048], BF16, "junkg")
        vn = k.sb([128, 2048], BF16, "vn")
        yg = k.sb([128, 2048], F32, "yg")
        ygn = k.sb([128, 2048], BF16, "ygn")
        msum = k.sb([128, 1], F32, "msum")
        ssq = k.sb([128, 1], F32, "ssq")
        rstd = k.sb([128, 1], F32, "rstd")
        ygT = [k.sb([128, 16, 128], BF16, "ygT%d" % i) for i in range(2)]
        svps = k.ps([128, 2048], F32, "svps")
        for tt in range(NT):
            u = us[tt % 2]
            v = vs[tt % 2]
            k.dma("sp", u[:, :], self.zg_d[tt * 128:(tt + 1) * 128, 0:2048], u)
            k.dma("sp", v[:, :], self.zg_d[tt * 128:(tt + 1) * 128, 2048:4096], v)
            k.op("dve", lambda: nc.vector.reduce_sum(out=msum[:, :], in_=v[:, :], axis=AX.X), r=[v], w=[msum])
            k.op("dve", lambda: nc.vector.tensor_scalar(out=msum[:, :], in0=msum[:, :], scalar1=-1.0 / 2048,
                                                        scalar2=None, op0=ALU.mult), r=[msum], w=[msum])
            k.op("act", lambda: nc.scalar.activation(out=vc[:, :], in_=v[:, :], func=AF.Identity,
                                                     bias=msum[:, 0:1], scale=1.0), r=[v, msum], w=[vc])
            self.sumsq(vc[:, :], [vc], junk[:, :], junk, ssq)
            self.rstd_from_ssq(ssq, rstd, 2048)
            k.op("dve", lambda: nc.vector.scalar_tensor_tensor(out=vn[:, :], in0=vc[:, :], scalar=rstd[:, 0:1],
                                                               in1=gs_bc[:, :], op0=ALU.mult, op1=ALU.mult),
                 r=[vc, rstd, gs_bc], w=[vn])
            for h in range(16):
                k.op("pe", lambda: nc.tensor.matmul(svps[:, h * 128:(h + 1) * 128], lhsT=WmT[:, h, :],
                                                    rhs=vn[:, h * 128:(h + 1) * 128], start=True, stop=True),
                     r=[WmT, vn], w=[svps])
            ygv = yg[:, :].rearrange("p (h d) -> p h d", h=16)
            k.op("dve", lambda: nc.vector.tensor_tensor(
                out=ygv, in0=svps[:, :].rearrange("p (h d) -> p h d", h=16),
                in1=bT[:, :].unsqueeze(2).to_broadcast([128, 16, 128]), op=ALU.add), r=[svps, bT], w=[yg])
            k.op("pool", lambda: nc.gpsimd.tensor_tensor(out=yg[:, :], in0=yg[:, :], in1=u[:, :], op=ALU.mult),
                 r=[yg, u], w=[yg])
            yT = ygT[tt % 2]
            self.norm_featmajor(yg, 2048, bgm, None, yT, 0, junk, ygn, ssq, rstd)
            k.dma("sp", self.yT_d[:, :, tt * 128:(tt + 1) * 128].rearrange("c p t -> p c t"), yT[:, :, :], yT,
                  store=True)
        k.end()

    def phase_DE(self):
        k, nc = self.k, self.nc
        k.begin()
        yT = k.sb([128, 32, 1024], BF16, "yT")
        masks = k.sb([128, 32, 128], BF16, "masks")
        rstd_m = k.sb([128, NT], F32, "rstd_m")
        k.dma("sp", yT[:, 16:32, :], self.yT_d.rearrange("c p t -> p c t"), yT)
        k.begin()
        qidx_bc = k.sb([128, 1024], F32, "qidx_bc")
        pio = k.sb([128, 1], F32, "pio")
        ones_b = k.sb([128, 128], BF16, "ones_b")
        ones_f = k.sb([128, 1], F32, "ones_f")
        bmla = k.sb([128, 16], F32, "bmla")
        k.dma("sp", qidx_bc[:, :], self.qidx.to_broadcast([128, 1024]), qidx_bc)
        k.dma("sp", pio[:, :], self.piota, pio)
        k.dma("sp", bmla[:, :], self.bmla, bmla)
        k.op("pool", lambda: nc.gpsimd.memset(ones_b[:, :], 1.0), w=[ones_b])
        k.op("pool", lambda: nc.gpsimd.memset(ones_f[:, :], 1.0), w=[ones_f])
        for s in range(NT):
            for r4 in range(4):
                kb = 4 * s + r4
                k.op("dve", lambda: nc.vector.tensor_scalar(out=masks[:, s * 4 + r4, :],
                                                            in0=qidx_bc[:, s * 128:(s + 1) * 128],
                                                            scalar1=float(-kb * 128), scalar2=pio[:, 0:1],
                                                            op0=ALU.add, op1=ALU.is_ge),
                     r=[qidx_bc, pio], w=[masks])
        kTs = [k.sb([128, S], BF16, "kTh%d" % i) for i in range(2)]
        Vs = [k.sb([128, 32, 128], BF16, "Vh%d" % i) for i in range(2)]
        qTs = [k.sb([128, 1024], BF16, "qTh%d" % i) for i in range(2)]
        qpTs = [k.sb([64, 1024], BF16, "qpTh%d" % i) for i in range(2)]
        pTs = [k.sb([128, 512], BF16, "pT%d" % i) for i in range(2)]
        rsum = k.sb([128, 1024], F32, "rsum")
        On = k.sb([128, 1024], F32, "On")
        sq = k.sb([128, 1024], F32, "sq")
        sps = [k.ps([128, 512], F32, "sps%d" % i) for i in range(2)]
        opss = [k.ps([128, 128], F32, "ops%d" % i) for i in range(2)]
        sumss = [k.ps([128, 128], F32, "sums%d" % i) for i in range(2)]
        ssqps = k.ps([128, NT], F32, "ssqps")
        ssq_acc = k.sb([128, NT], F32, "ssq_acc")
        scale = 1.0 / math.sqrt(192.0)
        groups = [(h, sl, g4) for h in range(16) for sl in range(NT) for g4 in range(sl + 1)]

        def load_head(h):
            k.dma("sp", kTs[h % 2][:, :], self.kT_d[h], kTs[h % 2])
            k.dma("sp", Vs[h % 2][:, :, :],
                  self.v_d[:, h * 128:(h + 1) * 128].rearrange("(kb p) d -> p kb d", p=128), Vs[h % 2])
            k.dma("sp", qTs[h % 2][:, :], self.qT_d[h], qTs[h % 2])
            k.dma("sp", qpTs[h % 2][:, :], self.qpeT_d[h], qpTs[h % 2])

        def emit_qk(idx):
            h, sl, g4 = groups[idx]
            kT, qT, qpT = kTs[h % 2], qTs[h % 2], qpTs[h % 2]
            sp_ = sps[idx % 2]
            q0 = sl * 128
            for i4 in range(4):
                kb = g4 * 4 + i4
                k.op("pe", lambda: nc.tensor.matmul(sp_[:, i4 * 128:(i4 + 1) * 128],
                                                    lhsT=kT[:, kb * 128:(kb + 1) * 128],
                                                    rhs=qT[:, q0:q0 + 128], start=True, stop=False),
                     r=[kT, qT], w=[sp_])
                k.op("pe", lambda: nc.tensor.matmul(sp_[:, i4 * 128:(i4 + 1) * 128],
                                                    lhsT=self.kpeT[0:64, kb * 128:(kb + 1) * 128],
                                                    rhs=qpT[0:64, q0:q0 + 128], start=False, stop=True),
                     r=[self.kpeT, qpT], w=[sp_])

        def emit_rest(idx):
            h, sl, g4 = groups[idx]
            Vh = Vs[h % 2]
            sp_ = sps[idx % 2]
            pT = pTs[idx % 2]
            q0 = sl * 128
            ns = h * NT + sl
            ops = opss[ns % 2]
            sums = sumss[ns % 2]
            ng4 = sl + 1
            k.op("act", lambda: nc.scalar.activation(out=pT[:, 0:512], in_=sp_[:, :], func=AF.Exp,
                                                     scale=scale), r=[sp_], w=[pT])
            if g4 == sl:
                k.op("dve", lambda: nc.vector.tensor_tensor(
                    out=pT[:, 0:512], in0=pT[:, 0:512],
                    in1=masks[:, sl * 4:(sl + 1) * 4, :].rearrange("p a b -> p (a b)"), op=ALU.mult),
                     r=[pT, masks], w=[pT])
            for i4 in range(4):
                kb = g4 * 4 + i4
                k.op("pe", lambda: nc.tensor.matmul(ops[:, :], lhsT=Vh[:, kb, :],
                                                    rhs=pT[:, i4 * 128:(i4 + 1) * 128],
                                                    start=(kb == 0), stop=(kb == 4 * ng4 - 1)),
                     r=[Vh, pT], w=[ops])
            for i4 in range(4):
                kb = g4 * 4 + i4
                k.op("pe", lambda: nc.tensor.matmul(sums[:, :], lhsT=ones_b[:, :],
                                                    rhs=pT[:, i4 * 128:(i4 + 1) * 128],
                                                    start=(kb == 0), stop=(kb == 4 * ng4 - 1)),
                     r=[ones_b, pT], w=[sums])
            if g4 != sl:
                return
            k.op("dve", lambda: nc.vector.reciprocal(out=rsum[:, q0:q0 + 128], in_=sums[:, :]),
                 r=[sums], w=[rsum])
            k.op("dve", lambda: nc.vector.tensor_tensor(out=On[:, q0:q0 + 128], in0=ops[:, :],
                                                        in1=rsum[:, q0:q0 + 128], op=ALU.mult),
                 r=[ops, rsum], w=[On])
            if sl != NT - 1:
                return
            k.op("act", lambda: nc.scalar.activation(out=sq[:, :], in_=On[:, :], func=AF.Square), r=[On], w=[sq])
            for tt in range(NT):
                k.op("pe", lambda: nc.tensor.matmul(ssqps[:, tt:tt + 1], lhsT=sq[:, tt * 128:(tt + 1) * 128],
                                                    rhs=ones_f[:, 0:1], start=True, stop=True),
                     r=[sq, ones_f], w=[ssqps])
            if h == 0:
                k.op("dve", lambda: nc.vector.tensor_copy(out=ssq_acc[:, :], in_=ssqps[:, :]),
                     r=[ssqps], w=[ssq_acc])
            else:
                k.op("dve", lambda: nc.vector.tensor_tensor(out=ssq_acc[:, :], in0=ssq_acc[:, :], in1=ssqps[:, :],
                                                            op=ALU.add), r=[ssqps, ssq_acc], w=[ssq_acc])
            k.op("dve", lambda: nc.vector.tensor_scalar(out=yT[:, h, :], in0=On[:, :], scalar1=bmla[:, h:h + 1],
                                                        scalar2=None, op0=ALU.mult), r=[On, bmla], w=[yT])
            if h + 2 < 16:
                load_head(h + 2)

        load_head(0)
        load_head(1)
        emit_qk(0)
        for idx in range(len(groups)):
            if idx + 1 < len(groups):
                emit_qk(idx + 1)
            emit_rest(idx)
        k.op("act", lambda: nc.scalar.activation(out=rstd_m[:, :], in_=ssq_acc[:, :], func=AF.Sqrt,
                                                 scale=1.0 / 2048, bias=self.epst[:, 0:1]),
             r=[ssq_acc, self.epst], w=[rstd_m])
        k.op("dve", lambda: nc.vector.reciprocal(out=rstd_m[:, :], in_=rstd_m[:, :]), r=[rstd_m], w=[rstd_m])
        if self.debug:
            dbg_y = nc.dram_tensor("dbg_yT", [128, 16, 1024], BF16, kind="ExternalOutput").ap()
            dbg_r = nc.dram_tensor("dbg_rstd", [128, NT], F32, kind="ExternalOutput").ap()
            dbg_m = nc.dram_tensor("dbg_masks", [128, 32, 128], BF16, kind="ExternalOutput").ap()
            dbg_s = nc.dram_tensor("dbg_rsum", [128, 1024], F32, kind="ExternalOutput").ap()
            k.dma("sp", dbg_y, yT[:, 0:16, :], yT, store=True)
            k.dma("sp", dbg_r, rstd_m[:, :], rstd_m, store=True)
            k.dma("sp", dbg_m, masks[:, :, :], masks, store=True)
            k.dma("sp", dbg_s, rsum[:, :], rsum, store=True)
        k.end()
        k.begin()
        gta = k.sb([128, D], F32, "gta")
        self.mod_bc(gta, 2)
        wps = [k.sb([128, 32, 512], BF16, "wo%d" % i) for i in range(2)]
        ps1 = [k.ps([128, 512], F32, "ps1_%d" % i) for i in range(2)]
        ps2 = [k.ps([128, 512], F32, "ps2_%d" % i) for i in range(2)]
        t2s = [k.sb([128, 512], F32, "t2_%d" % i) for i in range(2)]
        t1s = [k.sb([128, 512], F32, "t1_%d" % i) for i in range(2)]
        xts = [k.sb([128, 512], F32, "xe%d" % i) for i in range(2)]
        n = 0
        for cb in range(8):
            wp = wps[cb % 2]
            for q4 in range(4):
                k.dma("pool", wp[:, q4 * 8:(q4 + 1) * 8, :], self.wout[cb, :, q4 * 8:(q4 + 1) * 8, :], wp)
            for tt in range(NT):
                p1 = ps1[n % 2]
                p2 = ps2[n % 2]
                t2 = t2s[n % 2]
                t1 = t1s[n % 2]
                xt = xts[n % 2]
                n += 1
                k.dma("sp", xt[:, :], self.xq[tt * 128:(tt + 1) * 128, cb * 512:(cb + 1) * 512], xt)
                for kc in range(16):
                    k.op("pe", lambda: nc.tensor.matmul(p1[:, :], lhsT=yT[:, kc, tt * 128:(tt + 1) * 128],
                                                        rhs=wp[:, kc, :], start=(kc == 0), stop=(kc == 15)),
                         r=[yT, wp], w=[p1])
                for kc in range(16, 32):
                    k.op("pe", lambda: nc.tensor.matmul(p2[:, :], lhsT=yT[:, kc, tt * 128:(tt + 1) * 128],
                                                        rhs=wp[:, kc, :], start=(kc == 16), stop=(kc == 31)),
                         r=[yT, wp], w=[p2])
                k.op("act", lambda: nc.scalar.copy(out=t2[:, :], in_=p2[:, :]), r=[p2], w=[t2])
                k.op("dve", lambda: nc.vector.scalar_tensor_tensor(out=t1[:, :], in0=p1[:, :],
                                                                   scalar=rstd_m[:, tt:tt + 1], in1=t2[:, :],
                                                                   op0=ALU.mult, op1=ALU.add),
                     r=[p1, rstd_m, t2], w=[t1])
                k.op("pool", lambda: nc.gpsimd.tensor_tensor(out=t1[:, :], in0=t1[:, :],
                                                             in1=gta[:, cb * 512:(cb + 1) * 512], op=ALU.mult),
                     r=[t1, gta], w=[t1])
                k.op("dve", lambda: nc.vector.tensor_tensor(out=t1[:, :], in0=t1[:, :], in1=xt[:, :], op=ALU.add),
                     r=[t1, xt], w=[t1])
                k.dma("sp", self.x1_d[tt * 128:(tt + 1) * 128, cb * 512:(cb + 1) * 512], t1[:, :], t1, store=True)
        k.end()
        k.end()

    def phase_F(self):
        k, nc = self.k, self.nc
        for g in range(2):
            k.begin()
            Facc = k.sb([128, 4, D], F32, "Facc")
            k.begin()
            hnT = k.sb([128, 32, 512], BF16, "hnT")
            s_sb = k.sb([128, 4, 2048], F32, "s_sb")
            tau = k.sb([128, 4, 8], F32, "tau")
            gbias = k.sb([128, 4, 8], F32, "gbias")
            k.begin()
            self.make_tp()
            xts = [k.sb([128, D], F32, "xf%d" % i) for i in range(2)]
            junk = k.sb([128, D], BF16, "junk")
            xn = [k.sb([128, D], BF16, "xn%d" % i) for i in range(2)]
            ssq = [k.sb([128, 1], F32, "ssq%d" % i) for i in range(2)]
            rstd = [k.sb([128, 1], F32, "rstd%d" % i) for i in range(2)]
            for t4 in range(4):
                tt = g * 4 + t4
                xt = xts[t4 % 2]
                k.dma("sp", xt[:, :], self.x1_d[tt * 128:(tt + 1) * 128, :], xt)
                self.norm_featmajor(xt, D, self.Affn, self.modT, hnT, t4 * 128, junk, xn, ssq, rstd, boff=96)
            k.end()
            k.begin()
            keysT = k.sb([128, 16, 128], F32, "keysT")
            k.dma("sp", keysT[:, :, :], self.keysT, keysT)
            qchs = [k.sb([128, 512], F32, "qch%d" % i) for i in range(2)]
            wps = [k.sb([128, 32, 256], BF16, "wq%d" % i) for i in range(2)]
            psq = [k.ps([128, 512], F32, "psq%d" % i) for i in range(2)]
            sps4 = [k.ps([128, 512], F32, "sps4_%d" % i) for i in range(2)]
            n = 0
            for cb in range(8):
                wp = wps[cb % 2]
                for q4 in range(4):
                    k.dma("pool", wp[:, q4 * 8:(q4 + 1) * 8, :], self.wpq[cb, :, q4 * 8:(q4 + 1) * 8, :], wp)
                for c4 in range(2):
                    ch = cb * 2 + c4
                    ps = psq[n % 2]
                    qch = qchs[n % 2]
                    sp4 = sps4[n % 2]
                    n += 1
                    for kc in range(32):
                        k.op("pe", lambda: nc.tensor.matmul(ps[:, :], lhsT=wp[:, kc, c4 * 128:(c4 + 1) * 128],
                                                            rhs=hnT[:, kc, :], start=(kc == 0), stop=(kc == 31)),
                             r=[wp, hnT], w=[ps])
                    k.op("dve", lambda: nc.vector.tensor_copy(out=qch[:, :], in_=ps[:, :]), r=[ps], w=[qch])
                    for t4 in range(4):
                        k.op("pe", lambda: nc.tensor.matmul(sp4[:, t4 * 128:(t4 + 1) * 128],
                                                            lhsT=qch[:, t4 * 128:(t4 + 1) * 128],
                                                            rhs=keysT[:, ch, :], start=True, stop=True),
                             r=[qch, keysT], w=[sp4])
                    k.op("act", lambda: nc.scalar.copy(out=s_sb[:, :, ch * 128:(ch + 1) * 128],
                                                       in_=sp4[:, :].rearrange("p (t n) -> p t n", t=4)),
                         r=[sp4], w=[s_sb])
            sv = k.sb([128, 16, 16], F32, "sv")
            wk = k.sb([128, 128], F32, "wk")
            cand = k.sb([128, 8, 256], F32, "cand")
            wk2 = k.sb([128, 256], F32, "wk2")
            v24 = k.sb([128, 8, 24], F32, "v24")
            negm = k.sb([128, 8], F32, "negm")
            ez = k.sb([128, 8, 16], F32, "ez")
            zs = k.sb([128, 8], F32, "zs")
            for t4 in range(4):
                for ch in range(16):
                    sl = s_sb[:, t4, ch * 128:(ch + 1) * 128]
                    k.op("dve", lambda: nc.vector.max(out=sv[:, ch, 0:8], in_=sl), r=[s_sb], w=[sv])
                    k.op("dve", lambda: nc.vector.match_replace(out=wk[:, :], in_to_replace=sv[:, ch, 0:8],
                                                                in_values=sl, imm_value=-1e30),
                         r=[sv, s_sb], w=[wk])
                    k.op("dve", lambda: nc.vector.max(out=sv[:, ch, 8:16], in_=wk[:, :]), r=[wk], w=[sv])
                svv = sv[:, :, :].rearrange("p (h two) a -> p h two a", two=2)
                k.op("dve", lambda: nc.vector.tensor_tensor(
                    out=cand[:, :, :].rearrange("p h (a b) -> p h a b", a=16),
                    in0=svv[:, :, 0, :].unsqueeze(3).to_broadcast([128, 8, 16, 16]),
                    in1=svv[:, :, 1, :].unsqueeze(2).to_broadcast([128, 8, 16, 16]), op=ALU.add),
                     r=[sv], w=[cand])
                for h in range(8):
                    k.op("dve", lambda: nc.vector.max(out=v24[:, h, 0:8], in_=cand[:, h, :]), r=[cand], w=[v24])
                    k.op("dve", lambda: nc.vector.match_replace(out=wk2[:, :], in_to_replace=v24[:, h, 0:8],
                                                                in_values=cand[:, h, :], imm_value=-1e30),
                         r=[v24, cand], w=[wk2])
                    k.op("dve", lambda: nc.vector.max(out=v24[:, h, 8:16], in_=wk2[:, :]), r=[wk2], w=[v24])
                    k.op("dve", lambda: nc.vector.match_replace(out=wk2[:, :], in_to_replace=v24[:, h, 8:16],
                                                                in_values=wk2[:, :], imm_value=-1e30),
                         r=[v24, wk2], w=[wk2])
                    k.op("dve", lambda: nc.vector.max(out=v24[:, h, 16:24], in_=wk2[:, :]), r=[wk2], w=[v24])
                k.op("dve", lambda: nc.vector.tensor_tensor(out=tau[:, t4, :], in0=v24[:, :, 15], in1=v24[:, :, 16],
                                                            op=ALU.add), r=[v24], w=[tau])
                k.op("dve", lambda: nc.vector.tensor_scalar(out=tau[:, t4, :], in0=tau[:, t4, :], scalar1=0.5,
                                                            scalar2=None, op0=ALU.mult), r=[tau], w=[tau])
                k.op("dve", lambda: nc.vector.tensor_scalar(out=negm[:, :], in0=v24[:, :, 0], scalar1=-1.0,
                                                            scalar2=None, op0=ALU.mult), r=[v24], w=[negm])
                k.op("dve", lambda: nc.vector.tensor_tensor(out=ez[:, :, :], in0=v24[:, :, 0:16],
                                                            in1=negm[:, :].unsqueeze(2).to_broadcast([128, 8, 16]),
                                                            op=ALU.add), r=[v24, negm], w=[ez])
                k.op("act", lambda: nc.scalar.activation(out=ez[:, :, :], in_=ez[:, :, :], func=AF.Exp),
                     r=[ez], w=[ez])
                k.op("dve", lambda: nc.vector.reduce_sum(out=zs[:, :], in_=ez[:, :, :], axis=AX.X), r=[ez], w=[zs])
                k.op("act", lambda: nc.scalar.activation(out=zs[:, :], in_=zs[:, :], func=AF.Ln), r=[zs], w=[zs])
                k.op("dve", lambda: nc.vector.tensor_tensor(out=gbias[:, t4, :], in0=negm[:, :], in1=zs[:, :],
                                                            op=ALU.subtract), r=[negm, zs], w=[gbias])
            k.end()
            if self.stop_after == "F2":
                k.end()
                k.end()
                return
            k.begin()
            Us = [k.sb([128, 32, 128], BF16, "U%d" % i) for i in range(2)]
            Vp = [k.sb([128, 8, 512], BF16, "V%d" % i) for i in range(2)]
            gat = k.sb([128, 8, 512], BF16, "GAT")
            ATs = [k.sb([128, 512], BF16, "AT%d" % i) for i in range(2)]
            NB = 2
            cs = [k.sb([128, 8, 128], F32, "cs%d" % i) for i in range(NB)]
            Es = [k.sb([128, 8, 128], BF16, "Es%d" % i) for i in range(NB)]
            Ghs = [k.sb([128, 8, 128], BF16, "Gh%d" % i) for i in range(NB)]
            Gacc = [[k.sb([128, 8, 128], BF16, "Gacc%d_%d" % (j, i)) for i in range(4)] for j in range(2)]
            psA = [k.ps([128, 512], F32, "psA%d" % i) for i in range(2)]
            psF = [k.ps([128, 512], F32, "psF%d" % i) for i in range(4)]
            psG = [k.ps([128, 512], F32, "psG%d" % i) for i in range(2)]
            st = {"ng": 0, "na": 0, "nf": 0, "nt": 0, "issued": 0, "nU": 0, "nV": 0}
            pieces = []
            for eb in range(16):
                pieces += [("U", eb, c) for c in range(8)]
                pieces += [("V", eb, db) for db in range(8)]
            bufmap = {}

            def ensure(i):
                while st["issued"] <= min(i, len(pieces) - 1):
                    kind, eb_, j = pieces[st["issued"]]
                    if kind == "U":
                        t = Us[st["nU"] % 2]
                        st["nU"] += 1
                        k.dma("pool", t[:, :, :], self.U[eb_ * 8 + j], t)
                    else:
                        t = Vp[st["nV"] % 2]
                        st["nV"] += 1
                        k.dma("pool", t[:, :, :], self.V[eb_, j], t)
                    bufmap[st["issued"]] = t
                    st["issued"] += 1

            def emit_G(eb, part):
                t4 = part // 4
                ga = Gacc[eb % 2][t4]
                for h in range((part % 4) * 2, (part % 4) * 2 + 2):
                    cc = cs[st["ng"] % NB]
                    Ee = Es[st["ng"] % NB]
                    Gh = Ghs[st["ng"] % NB]
                    st["ng"] += 1
                    s1 = s_sb[:, t4, (2 * h) * 128 + eb * 8:(2 * h) * 128 + eb * 8 + 8]
                    s2 = s_sb[:, t4, (2 * h + 1) * 128:(2 * h + 2) * 128]
                    k.op("pool", lambda: nc.gpsimd.tensor_tensor(
                        out=cc[:, :, :], in0=s1.unsqueeze(2).to_broadcast([128, 8, 128]),
                        in1=s2.unsqueeze(1).to_broadcast([128, 8, 128]), op=ALU.add), r=[s_sb], w=[cc])
                    k.op("act", lambda: nc.scalar.activation(out=Ee[:, :, :], in_=cc[:, :, :], func=AF.Exp,
                                                             bias=gbias[:, t4, h:h + 1], scale=1.0),
                         r=[cc, gbias], w=[Ee])
                    dst = ga if h == 0 else Gh
                    k.op("dve", lambda: nc.vector.scalar_tensor_tensor(
                        out=dst[:, :, :], in0=cc[:, :, :], scalar=tau[:, t4, h:h + 1], in1=Ee[:, :, :],
                        op0=ALU.is_ge, op1=ALU.mult), r=[cc, tau, Ee], w=[dst])
                    if h > 0:
                        k.op("pool", lambda: nc.gpsimd.tensor_tensor(out=ga[:, :, :], in0=ga[:, :, :],
                                                                     in1=Gh[:, :, :], op=ALU.add),
                             r=[ga, Gh], w=[ga])

            for part in range(16):
                emit_G(0, part)
            pi = 0
            for eb in range(16):
                for c in range(8):
                    ensure(pi + 1)
                    Ut = bufmap.pop(pi)
                    pi += 1
                    pa = psA[st["na"] % 2]
                    at = ATs[st["na"] % 2]
                    st["na"] += 1
                    for kc in range(32):
                        k.op("pe", lambda: nc.tensor.matmul(pa[:, :], lhsT=Ut[:, kc, :], rhs=hnT[:, kc, :],
                                                            start=(kc == 0), stop=(kc == 31)), r=[Ut, hnT], w=[pa])
                    k.op("act", lambda: nc.scalar.activation(out=at[:, :], in_=pa[:, :], func=AF.Gelu),
                         r=[pa], w=[at])
                    pg = psG[st["nt"] % 2]
                    st["nt"] += 1
                    for t4 in range(4):
                        ga = Gacc[eb % 2][t4]
                        k.op("pe", lambda: nc.tensor.matmul(pg[:, t4 * 128:(t4 + 1) * 128], lhsT=ga[:, c, :],
                                                            rhs=self.identb[:, :], start=True, stop=True),
                             r=[ga, self.identb], w=[pg])
                    k.op("dve", lambda: nc.vector.tensor_tensor(out=gat[:, c, :], in0=pg[:, :], in1=at[:, :],
                                                                op=ALU.mult), r=[pg, at], w=[gat])
                    if eb + 1 < 16:
                        emit_G(eb + 1, c)
                for db in range(8):
                    ensure(pi + 1)
                    Vt = bufmap.pop(pi)
                    pi += 1
                    for t4 in range(4):
                        pf = psF[st["nf"] % 4]
                        st["nf"] += 1
                        for c in range(8):
                            k.op("pe", lambda: nc.tensor.matmul(pf[:, :], lhsT=gat[:, c, t4 * 128:(t4 + 1) * 128],
                                                                rhs=Vt[:, c, :], start=(c == 0), stop=(c == 7)),
                                 r=[gat, Vt], w=[pf])
                        fa = Facc[:, t4, db * 512:(db + 1) * 512]
                        if eb == 0:
                            k.op("dve", lambda: nc.vector.tensor_copy(out=fa, in_=pf[:, :]), r=[pf], w=[Facc])
                        else:
                            k.op("dve", lambda: nc.vector.tensor_tensor(out=fa, in0=fa, in1=pf[:, :], op=ALU.add),
                                 r=[pf, Facc], w=[Facc])
                    if eb + 1 < 16:
                        emit_G(eb + 1, 8 + db)
            k.end()
            k.end()
            k.begin()
            gtf = k.sb([128, D], F32, "gtf")
            Afin = k.sb([128, D], F32, "Afin")
            Bfin = k.sb([128, D], F32, "Bfin")
            gfb = k.sb([128, D], F32, "gfb")
            self.mod_bc(gtf, 5)
            self.mod_bc(Bfin, 6)
            self.mod_bc(Afin, 7)
            k.dma("sp", gfb[:, :], self.gfin.to_broadcast([128, D]), gfb)
            k.op("dve", lambda: nc.vector.scalar_tensor_tensor(out=Afin[:, :], in0=Afin[:, :], scalar=1.0,
                                                               in1=gfb[:, :], op0=ALU.add, op1=ALU.mult),
                 r=[Afin, gfb], w=[Afin])
            x1s = [k.sb([128, D], F32, "x1_%d" % i) for i in range(2)]
            junk = gfb
            ssq = k.sb([128, 1], F32, "ssq")
            rstd = k.sb([128, 1], F32, "rstd")
            for t4 in range(4):
                tt = g * 4 + t4
                x1 = x1s[t4 % 2]
                k.dma("sp", x1[:, :], self.x1_d[tt * 128:(tt + 1) * 128, :], x1)
                k.op("pool", lambda: nc.gpsimd.tensor_tensor(out=Facc[:, t4, :], in0=Facc[:, t4, :], in1=gtf[:, :],
                                                             op=ALU.mult), r=[Facc, gtf], w=[Facc])
                k.op("dve", lambda: nc.vector.tensor_tensor(out=x1[:, :], in0=x1[:, :], in1=Facc[:, t4, :],
                                                            op=ALU.add), r=[x1, Facc], w=[x1])
                self.sumsq(x1[:, :], [x1], junk[:, :], junk, ssq)
                self.rstd_from_ssq(ssq, rstd, D)
                k.op("dve", lambda: nc.vector.scalar_tensor_tensor(out=x1[:, :], in0=x1[:, :], scalar=rstd[:, 0:1],
                                                                   in1=Afin[:, :], op0=ALU.mult, op1=ALU.mult),
                     r=[x1, rstd, Afin], w=[x1])
                k.op("pool", lambda: nc.gpsimd.tensor_tensor(out=x1[:, :], in0=x1[:, :], in1=Bfin[:, :], op=ALU.add),
                     r=[x1, Bfin], w=[x1])
                k.dma("sp", self.out[tt * 128:(tt + 1) * 128, :], x1[:, :], x1, store=True)
            k.end()
            k.end()

    def build(self):
        self.declare()
        self.phase_setup()
        sa = self.stop_after
        self.phase_A()
        if sa == "A":
            self.k.finish()
            return self.nc
        self.k.begin()
        self.kpeT = self.k.sb([64, S], BF16, "kpeT")
        self.phase_B()
        if sa != "B":
            self.phase_C()
            if sa not in ("C2", "C3", "C"):
                self.phase_DE()
        self.k.end()
        if sa in ("B", "C2", "C3", "C", "DE"):
            self.k.finish()
            return self.nc
        self.phase_F()
        self.k.finish()
        return self.nc


def _fm(v, n):
    return np.ascontiguousarray(np.asarray(v, np.float32).reshape(n, 128).T)


def _blocks(w, ncol):
    K, C = w.shape
    return np.ascontiguousarray(w.reshape(K // 128, 128, C // ncol, ncol).transpose(2, 1, 0, 3))


def _own_blocks(j):
    blks = []
    for m in range(4):
        blks.append(8 * m + j)
        blks.append(8 * m + 7 - j)
    return blks


_PROG = None


def _shared_inputs(w_ada, b_ada, g_norm_mix, w_in, g_q, w_uq, g_kv, w_ukv, g_sgu, w_sgu, b_sgu, beta_mla,
                   beta_gmlp, w_out, g_norm_ffn, w_pq, peer_keys, expert_u, expert_v, w_ada_f, b_ada_f, g_norm_f):
    f = np.float32
    sh = {}
    wa = np.concatenate([np.asarray(w_ada[0], f), np.asarray(w_ada_f, f)], axis=1)
    sh["wada"] = _blocks(wa, 256)
    del wa
    ba = np.concatenate([np.asarray(b_ada[0], f), np.asarray(b_ada_f, f)])
    sh["bada"] = _fm(ba, 256)
    sh["gmix"] = _fm(g_norm_mix[0], 32)
    sh["gffn"] = _fm(g_norm_ffn[0], 32)
    sh["gfin"] = np.asarray(g_norm_f, f).reshape(1, D)
    wi = np.asarray(w_in[0], f)
    sh["wkv"] = np.ascontiguousarray(wi[:, 1024:1600].reshape(32, 128, 576).transpose(1, 0, 2))
    sh["wqg"] = _blocks(np.concatenate([wi[:, 0:1024], wi[:, 1600:5696]], axis=1), 512)
    sh["gq"] = _fm(g_q[0], 8)
    sh["gkv"] = _fm(g_kv[0], 4)
    sh["wuq"] = np.ascontiguousarray(np.asarray(w_uq[0], f).reshape(8, 128, 3072).transpose(1, 0, 2))
    wk = np.asarray(w_ukv[0], f).reshape(512, 16, 256)
    sh["wuk"] = np.ascontiguousarray(wk[:, :, 0:128].reshape(4, 128, 2048).transpose(1, 0, 2))
    sh["wuv"] = np.ascontiguousarray(wk[:, :, 128:256].reshape(4, 128, 2048).transpose(1, 0, 2))
    sh["gsgu"] = np.asarray(g_sgu[0], f).reshape(1, 2048)
    sh["wsgu"] = np.ascontiguousarray(np.asarray(w_sgu[0], f).transpose(2, 0, 1))
    sh["bsgu"] = np.ascontiguousarray(np.asarray(b_sgu[0], f).T)
    sh["bmla"] = _fm(beta_mla[0], 16)
    sh["bgm"] = _fm(beta_gmlp[0], 16)
    sh["wout"] = _blocks(np.asarray(w_out[0], f), 512)
    sh["wpq"] = _blocks(np.asarray(w_pq[0], f), 256)
    pk = np.asarray(peer_keys[0], f)
    sh["keysT"] = np.ascontiguousarray(pk.reshape(16, 128, 128).transpose(2, 0, 1))
    eu = np.asarray(expert_u[0], f)
    sh["U"] = np.ascontiguousarray(eu.reshape(128, 128, 32, 128).transpose(0, 3, 2, 1))
    ev = np.asarray(expert_v[0], f)
    sh["V"] = np.ascontiguousarray(ev.reshape(16, 8, 128, 8, 512).transpose(0, 3, 2, 1, 4))
    sh["invf"] = (np.float32(10000.0) ** (-np.arange(0, 64, 2, dtype=np.float32) / np.float32(64))).astype(
        f).reshape(1, 32)
    sh["piota"] = np.arange(128, dtype=f).reshape(128, 1)
    sh["ident"] = np.eye(128, dtype=f)
    sh["tril"] = np.triu(np.ones((128, 128), f))
    return sh


def _core_inputs(core, x, c, positions):
    f = np.float32
    b, j = core // 4, core % 4
    blks = _own_blocks(j)
    xb = np.asarray(x[b], f)
    rows = np.concatenate([np.arange(bl * 128, (bl + 1) * 128) for bl in blks])
    pos = np.asarray(positions[b], np.int32)
    m = {}
    m["xq"] = np.ascontiguousarray(xb[rows])
    m["xall"] = xb
    m["posq"] = np.ascontiguousarray(pos[rows].reshape(NT, 128).T)
    m["posall"] = np.ascontiguousarray(pos.reshape(32, 128).T)
    m["qidx"] = rows.astype(f).reshape(1, 1024)
    m["cvec"] = _fm(c[b], 32)
    return m, rows


def kernel(x, c, positions, w_ada, b_ada, g_norm_mix, w_in, g_q, w_uq, g_kv, w_ukv, g_sgu, w_sgu, b_sgu,
           beta_mla, beta_gmlp, w_out, g_norm_ffn, w_pq, peer_keys, expert_u, expert_v, w_ada_f, b_ada_f,
           g_norm_f):
    global _PROG
    x = np.asarray(x)
    shared = _shared_inputs(w_ada, b_ada, g_norm_mix, w_in, g_q, w_uq, g_kv, w_ukv, g_sgu, w_sgu, b_sgu,
                            beta_mla, beta_gmlp, w_out, g_norm_ffn, w_pq, peer_keys, expert_u, expert_v,
                            w_ada_f, b_ada_f, g_norm_f)
    in_maps = []
    rows_all = []
    for core in range(8):
        m, rows = _core_inputs(core, x, c, positions)
        m.update(shared)
        in_maps.append(m)
        rows_all.append(rows)
    nc = Prog().build()
    res = run_bass_kernel_spmd(nc, in_maps, core_ids=list(range(8)))
    out = np.empty((2, S, D), np.float32)
    for core in range(8):
        out[core // 4, rows_all[core], :] = res.results[core]["out"]
    return out
```

```python
from contextlib import ExitStack
import math
import numpy as np
import concourse.bass as bass
import concourse.mybir as mybir
from concourse.bass_utils import run_bass_kernel_spmd

F32 = mybir.dt.float32
BF16 = mybir.dt.bfloat16
I32 = mybir.dt.int32
ALU = mybir.AluOpType
AF = mybir.ActivationFunctionType
AX = mybir.AxisListType

D = 4096
S = 4096
NT = 8
EPS = 1e-6
TWO_PI = 2.0 * math.pi
C1 = 6.28125
C2 = TWO_PI - C1


class Tile:
    __slots__ = ("t", "w", "r", "sem", "dcnt", "name")

    def __init__(self, t, name):
        self.t = t
        self.name = name
        self.w = None
        self.r = {}
        self.sem = None
        self.dcnt = 0

    def __getitem__(self, k):
        return self.t[k]


class KB:
    ENG = ("pe", "dve", "act", "pool", "sp")

    def __init__(self, nc, n_dma_sems=80):
        self.nc = nc
        self.es = ExitStack()
        self.eng = {"pe": nc.tensor, "dve": nc.vector, "act": nc.scalar,
                    "pool": nc.gpsimd, "sp": nc.sync}
        self.sem = {}
        self.cnt = {}
        for e in self.ENG:
            self.sem[e] = self.es.enter_context(nc.semaphore("s_" + e))
            self.cnt[e] = 0
        self.seen = {e: {} for e in self.ENG}
        self.dsem = [self.es.enter_context(nc.semaphore("d%d" % i)) for i in range(n_dma_sems)]
        self.dval = [0] * n_dma_sems
        self.dfree = list(range(n_dma_sems))
        self.stacks = []
        self.tiles = []
        self.uid = 0
        self.ninst = 0
        self.nwaits = 0

    def begin(self):
        self.stacks.append(ExitStack())
        self.tiles.append([])

    def end(self):
        self.barrier()
        for t in self.tiles.pop():
            if t.sem is not None:
                self.dfree.append(t.sem)
                t.sem = None
        self.stacks.pop().close()

    def sb(self, shape, dtype, name):
        self.uid += 1
        st = self.stacks[-1] if self.stacks else self.es
        t = st.enter_context(self.nc.sbuf_tensor("%s_%d" % (name, self.uid), list(shape), dtype))
        tl = Tile(t, name)
        if self.stacks:
            self.tiles[-1].append(tl)
        return tl

    def ps(self, shape, dtype, name):
        self.uid += 1
        st = self.stacks[-1] if self.stacks else self.es
        esz = 2 if dtype == BF16 else 4
        assert len(shape) == 2
        per_bank = 2048 // esz
        ncol = ((shape[1] + per_bank - 1) // per_bank) * per_bank
        t = st.enter_context(self.nc.psum_tensor("%s_%d" % (name, self.uid), [shape[0], ncol], dtype))
        return Tile(t[:, 0:shape[1]], name)

    def alias(self, tile, name):
        return Tile(tile.t, name)

    def _semof(self, key):
        return self.sem[key] if isinstance(key, str) else self.dsem[key]

    def _need(self, e, ev, needs):
        if ev is None:
            return
        k, v = ev
        if self.seen[e].get(k, 0) >= v:
            return
        if needs.get(k, 0) < v:
            needs[k] = v

    def _emit_waits(self, e, needs):
        for k, v in needs.items():
            self.eng[e].wait_ge(self._semof(k), v)
            self.seen[e][k] = v
            self.nwaits += 1

    def _deps(self, e, r, w, needs):
        for t in r:
            self._need(e, t.w, needs)
        for t in w:
            if not (e == "pe" and t.w is not None and t.w[0] == "pe"):
                self._need(e, t.w, needs)
            for k, v in t.r.items():
                self._need(e, (k, v), needs)

    def op(self, e, fn, r=(), w=()):
        needs = {}
        self._deps(e, r, w, needs)
        self._emit_waits(e, needs)
        inst = fn()
        self.cnt[e] += 1
        inst.then_inc(self.sem[e], 1)
        ev = (e, self.cnt[e])
        for t in w:
            t.w = ev
            t.r = {}
        for t in r:
            if t not in w:
                t.r[e] = self.cnt[e]
        self.ninst += 1
        return inst

    def dma(self, q, out, in_, tile, store=False, **kw):
        needs = {}
        if store:
            self._deps(q, [tile], [], needs)
        else:
            self._deps(q, [], [tile], needs)
        self._emit_waits(q, needs)
        if tile.sem is None:
            tile.sem = self.dfree.pop()
        inst = self.eng[q].dma_start(out=out, in_=in_, **kw)
        inst.then_inc(self.dsem[tile.sem], 16)
        self.dval[tile.sem] += 16
        tile.dcnt = self.dval[tile.sem]
        if store:
            tile.r[tile.sem] = tile.dcnt
        else:
            tile.w = (tile.sem, tile.dcnt)
            tile.r = {}
        self.ninst += 1
        return inst

    def barrier(self):
        for e in self.ENG:
            needs = {}
            for k2 in self.ENG:
                if k2 != e and self.cnt[k2] > 0:
                    self._need(e, (k2, self.cnt[k2]), needs)
            for i, v in enumerate(self.dval):
                if v > 0:
                    self._need(e, (i, v), needs)
            self._emit_waits(e, needs)

    def finish(self):
        while self.stacks:
            self.end()
        self.barrier()
        self.es.close()


class Prog:
    def __init__(self, stop_after=None, debug=False):
        self.stop_after = stop_after
        self.debug = debug
        nc = bass.Bass("TRN2", target_bir_lowering=False)
        self.nc = nc
        self.k = KB(nc)
        self.din = {}
        self.rr = 0

    def inp(self, name, shape, dtype=F32):
        ap = self.nc.dram_tensor(name, list(shape), dtype, kind="ExternalInput").ap()
        self.din[name] = ap
        return ap

    def scratch(self, name, shape, dtype):
        return self.nc.dram_tensor(name, list(shape), dtype,
                                   kind=("ExternalOutput" if self.debug else "Internal")).ap()

    def declare(self):
        i = self.inp
        self.xq = i("xq", [1024, D])
        self.xall = i("xall", [S, D])
        self.posq = i("posq", [128, NT], I32)
        self.posall = i("posall", [128, 32], I32)
        self.qidx = i("qidx", [1, 1024])
        self.cvec = i("cvec", [128, 32])
        self.invf = i("invf", [1, 32])
        self.piota = i("piota", [128, 1])
        self.ident = i("ident", [128, 128])
        self.tril = i("tril", [128, 128])
        self.wada = i("wada", [128, 128, 32, 256])
        self.bada = i("bada", [128, 256])
        self.gmix = i("gmix", [128, 32])
        self.gffn = i("gffn", [128, 32])
        self.gfin = i("gfin", [1, D])
        self.wkv = i("wkv", [128, 32, 576])
        self.wqg = i("wqg", [10, 128, 32, 512])
        self.gq = i("gq", [128, 8])
        self.gkv = i("gkv", [128, 4])
        self.wuq = i("wuq", [128, 8, 3072])
        self.wuk = i("wuk", [128, 4, 2048])
        self.wuv = i("wuv", [128, 4, 2048])
        self.gsgu = i("gsgu", [1, 2048])
        self.wsgu = i("wsgu", [128, 16, 128])
        self.bsgu = i("bsgu", [128, 16])
        self.bmla = i("bmla", [128, 16])
        self.bgm = i("bgm", [128, 16])
        self.wout = i("wout", [8, 128, 32, 512])
        self.wpq = i("wpq", [8, 128, 32, 256])
        self.keysT = i("keysT", [128, 16, 128])
        self.U = i("U", [128, 128, 32, 128])
        self.V = i("V", [16, 8, 128, 8, 512])
        self.out = self.nc.dram_tensor("out", [1024, D], F32, kind="ExternalOutput").ap()
        s = self.scratch
        self.modrows = s("modrows", [256, 128], F32)
        self.kT_d = s("kT_d", [16, 128, S], BF16)
        self.v_d = s("v_d", [S, 2048], BF16)
        self.zq_d = s("zq_d", [1024, 1024], F32)
        self.zg_d = s("zg_d", [1024, 4096], F32)
        self.qT_d = s("qT_d", [16, 128, 1024], BF16)
        self.qpeT_d = s("qpeT_d", [16, 64, 1024], BF16)
        self.yT_d = s("yT_d", [16, 128, 1024], BF16)
        self.x1_d = s("x1_d", [1024, D], F32)

    def evac_engine(self):
        self.rr += 1
        return "dve" if (self.rr % 2) else "act"

    def rstd_from_ssq(self, ssq, rstd, n):
        k, nc = self.k, self.nc
        k.op("act", lambda: nc.scalar.activation(out=rstd[:, :], in_=ssq[:, :], func=AF.Sqrt,
                                                 scale=1.0 / n, bias=self.epst[:, 0:1]),
             r=[ssq, self.epst], w=[rstd])
        k.op("dve", lambda: nc.vector.reciprocal(out=rstd[:, :], in_=rstd[:, :]), r=[rstd], w=[rstd])

    def sumsq(self, src_ap, src_tiles, junk_ap, junk_tile, ssq):
        k, nc = self.k, self.nc
        k.op("pool", lambda: nc.gpsimd.memset(ssq[:, :], 0.0), w=[ssq])
        k.op("act", lambda: nc.scalar.activation(out=junk_ap, in_=src_ap, func=AF.Square,
                                                 accum_out=ssq[:, :]),
             r=src_tiles, w=[junk_tile, ssq])

    def transpose_evac(self, src_tile, src_ap, np_out, dst_tile, dst_ap, scale_ap=None, bias_ap=None,
                       extra_r=()):
        k, nc = self.k, self.nc
        slot = self.tp[self.tpi % len(self.tp)]
        self.tpi += 1
        k.op("pe", lambda: nc.tensor.transpose(out=slot[0:np_out, self.tpcol(slot)], in_=src_ap,
                                               identity=self.identb[:, :]),
             r=[src_tile, self.identb], w=[slot])
        e = self.evac_engine()
        pin = slot[0:np_out, self.tpcol(slot)]
        if scale_ap is None:
            if e == "dve":
                k.op("dve", lambda: nc.vector.tensor_copy(out=dst_ap, in_=pin), r=[slot], w=[dst_tile])
            else:
                k.op("act", lambda: nc.scalar.copy(out=dst_ap, in_=pin), r=[slot], w=[dst_tile])
        elif bias_ap is None:
            if e == "dve":
                k.op("dve", lambda: nc.vector.tensor_scalar(out=dst_ap, in0=pin, scalar1=scale_ap, scalar2=None,
                                                            op0=ALU.mult), r=[slot] + list(extra_r), w=[dst_tile])
            else:
                k.op("act", lambda: nc.scalar.activation(out=dst_ap, in_=pin, func=AF.Copy, scale=scale_ap),
                     r=[slot] + list(extra_r), w=[dst_tile])
        else:
            if e == "dve":
                k.op("dve", lambda: nc.vector.tensor_scalar(out=dst_ap, in0=pin, scalar1=scale_ap, scalar2=bias_ap,
                                                            op0=ALU.mult, op1=ALU.add),
                     r=[slot] + list(extra_r), w=[dst_tile])
            else:
                k.op("act", lambda: nc.scalar.activation(out=dst_ap, in_=pin, func=AF.Identity, scale=scale_ap,
                                                         bias=bias_ap),
                     r=[slot] + list(extra_r), w=[dst_tile])

    def tpcol(self, slot):
        return slice(0, 128)

    def make_tp(self, n=2):
        self.tp = [self.k.ps([128, 1024], BF16, "tpb%d" % i) for i in range(n)]
        self.tpi = 0

    def sincos(self, ang, shape, sin_t, cos_t, tmp_f, tmp_i):
        k, nc = self.k, self.nc

        def full(t):
            return t[tuple(slice(None) for _ in shape)]

        for (dst, shift) in ((sin_t, 0.0), (cos_t, math.pi / 2)):
            k.op("dve", lambda: nc.vector.tensor_scalar(out=full(tmp_f), in0=full(ang), scalar1=shift,
                                                        scalar2=1.0 / TWO_PI, op0=ALU.add, op1=ALU.mult),
                 r=[ang], w=[tmp_f])
            k.op("dve", lambda: nc.vector.tensor_copy(out=full(tmp_i), in_=full(tmp_f)), r=[tmp_f], w=[tmp_i])
            k.op("dve", lambda: nc.vector.tensor_copy(out=full(tmp_f), in_=full(tmp_i)), r=[tmp_i], w=[tmp_f])
            k.op("dve", lambda: nc.vector.scalar_tensor_tensor(out=full(dst), in0=full(tmp_f), scalar=-C1,
                                                               in1=full(ang), op0=ALU.mult, op1=ALU.add),
                 r=[tmp_f, ang], w=[dst])
            k.op("dve", lambda: nc.vector.scalar_tensor_tensor(out=full(dst), in0=full(tmp_f), scalar=-C2,
                                                               in1=full(dst), op0=ALU.mult, op1=ALU.add),
                 r=[tmp_f, dst], w=[dst])
            if shift != 0.0:
                k.op("dve", lambda: nc.vector.tensor_scalar(out=full(dst), in0=full(dst), scalar1=shift,
                                                            scalar2=None, op0=ALU.add), r=[dst], w=[dst])
            k.op("dve", lambda: nc.vector.tensor_scalar(out=full(dst), in0=full(dst), scalar1=math.pi,
                                                        scalar2=-math.pi, op0=ALU.min, op1=ALU.max),
                 r=[dst], w=[dst])
            k.op("act", lambda: nc.scalar.activation(out=full(dst), in_=full(dst), func=AF.Sin), r=[dst], w=[dst])

    def rope_tables(self, pos_dram, nblk, name):
        k, nc = self.k, self.nc
        posi = k.sb([128, nblk], I32, name + "posi")
        posf = k.sb([128, nblk], F32, name + "posf")
        invb = k.sb([128, 32], F32, name + "invb")
        ang = k.sb([128, nblk, 32], F32, name + "ang")
        tf = k.sb([128, nblk, 32], F32, name + "tf")
        ti = k.sb([128, nblk, 32], I32, name + "ti")
        sin_t = k.sb([128, nblk, 32], F32, name + "sin")
        cos_t = k.sb([128, nblk, 32], F32, name + "cos")
        k.dma("sp", posi[:, :], pos_dram, posi)
        k.dma("sp", invb[:, :], self.invf.to_broadcast([128, 32]), invb)
        k.op("dve", lambda: nc.vector.tensor_copy(out=posf[:, :], in_=posi[:, :]), r=[posi], w=[posf])
        k.op("dve", lambda: nc.vector.tensor_tensor(
            out=ang[:, :, :], in0=posf[:, :].unsqueeze(2).to_broadcast([128, nblk, 32]),
            in1=invb[:, :].unsqueeze(1).to_broadcast([128, nblk, 32]), op=ALU.mult), r=[posf, invb], w=[ang])
        self.sincos(ang, [128, nblk, 32], sin_t, cos_t, tf, ti)
        return sin_t, cos_t

    def rope_apply(self, x1, x2, cosb, sinb, o1, o2, shape, srcs, dst, tmps):
        k, nc = self.k, self.nc
        ta, tb = tmps

        def full(t):
            return t[tuple(slice(None) for _ in shape)]

        k.op("dve", lambda: nc.vector.tensor_tensor(out=full(ta), in0=x1, in1=cosb, op=ALU.mult), r=srcs, w=[ta])
        k.op("pool", lambda: nc.gpsimd.tensor_tensor(out=full(tb), in0=x2, in1=sinb, op=ALU.mult), r=srcs, w=[tb])
        k.op("dve", lambda: nc.vector.tensor_tensor(out=o1, in0=full(ta), in1=full(tb), op=ALU.subtract),
             r=[ta, tb], w=[dst])
        k.op("dve", lambda: nc.vector.tensor_tensor(out=full(ta), in0=x2, in1=cosb, op=ALU.mult), r=srcs, w=[ta])
        k.op("pool", lambda: nc.gpsimd.tensor_tensor(out=full(tb), in0=x1, in1=sinb, op=ALU.mult), r=srcs, w=[tb])
        k.op("dve", lambda: nc.vector.tensor_tensor(out=o2, in0=full(ta), in1=full(tb), op=ALU.add),
             r=[ta, tb], w=[dst])

    def norm_featmajor(self, xt, F, A, B, dstT, col0, junk, xn, ssq, rstd, boff=0):
        k, nc = self.k, self.nc
        self.nrm = getattr(self, "nrm", 0) + 1

        def pick(x):
            return x[self.nrm % len(x)] if isinstance(x, list) else x

        junk, xn, ssq, rstd = pick(junk), pick(xn), pick(ssq), pick(rstd)
        self.sumsq(xt[:, 0:F], [xt], junk[:, 0:F], junk, ssq)
        self.rstd_from_ssq(ssq, rstd, F)
        k.op("act", lambda: nc.scalar.activation(out=xn[:, 0:F], in_=xt[:, 0:F], func=AF.Copy, scale=rstd[:, 0:1]),
             r=[xt, rstd], w=[xn])
        for kc in range(F // 128):
            self.transpose_evac(xn, xn[:, kc * 128:(kc + 1) * 128], 128, dstT,
                                dstT[:, kc, col0:col0 + 128],
                                scale_ap=A[:, kc:kc + 1],
                                bias_ap=(B[:, boff + kc:boff + kc + 1] if B is not None else None),
                                extra_r=[A] + ([B] if B is not None else []))

    def phase_setup(self):
        k, nc = self.k, self.nc
        self.epst = k.sb([128, 1], F32, "epst")
        k.op("pool", lambda: nc.gpsimd.memset(self.epst[:, :], EPS), w=[self.epst])
        self.identf = k.sb([128, 128], F32, "identf")
        self.identb = k.sb([128, 128], BF16, "identb")
        k.dma("sp", self.identf[:, :], self.ident, self.identf)
        k.op("dve", lambda: nc.vector.tensor_copy(out=self.identb[:, :], in_=self.identf[:, :]),
             r=[self.identf], w=[self.identb])
        self.modT = k.sb([128, 256], F32, "modT")
        self.Amix = k.sb([128, 32], F32, "Amix")
        self.Affn = k.sb([128, 32], F32, "Affn")

    def phase_A(self):
        k, nc = self.k, self.nc
        k.begin()
        cT = k.sb([128, 32], F32, "cT")
        cact = k.sb([128, 32], F32, "cact")
        bT = k.sb([128, 256], F32, "bT")
        g1 = k.sb([128, 32], F32, "g1")
        g2 = k.sb([128, 32], F32, "g2")
        modps = k.ps([128, 256], F32, "modps")
        tpf = k.ps([128, 128], F32, "tpf")
        rows = k.sb([128, 128], F32, "rows")
        wts = [k.sb([128, 32, 256], BF16, "wadat%d" % i) for i in range(4)]
        cact_b = k.sb([128, 32], BF16, "cact_b")
        k.dma("sp", cT[:, :], self.cvec, cT)
        k.dma("sp", bT[:, :], self.bada, bT)
        k.dma("sp", g1[:, :], self.gmix, g1)
        k.dma("sp", g2[:, :], self.gffn, g2)
        k.op("act", lambda: nc.scalar.activation(out=cact[:, :], in_=cT[:, :], func=AF.Silu), r=[cT], w=[cact])
        k.op("dve", lambda: nc.vector.tensor_copy(out=cact_b[:, :], in_=cact[:, :]), r=[cact], w=[cact_b])
        for blk in range(128):
            wt = wts[blk % 4]
            k.dma("pool", wt[:, :, :], self.wada[blk], wt)
            for n2 in range(2):
                j = blk * 2 + n2
                for kc in range(32):
                    k.op("pe", lambda: nc.tensor.matmul(modps[:, j:j + 1], lhsT=wt[:, kc, n2 * 128:(n2 + 1) * 128],
                                                        rhs=cact_b[:, kc:kc + 1], start=(kc == 0), stop=(kc == 31)),
                         r=[wt, cact_b], w=[modps])
        modT = self.modT
        k.op("dve", lambda: nc.vector.tensor_tensor(out=modT[:, :], in0=modps[:, :], in1=bT[:, :], op=ALU.add),
             r=[modps, bT], w=[modT])
        for (A, g, c0) in ((self.Amix, g1, 32), (self.Affn, g2, 128)):
            k.op("dve", lambda: nc.vector.scalar_tensor_tensor(out=A[:, :], in0=modT[:, c0:c0 + 32], scalar=1.0,
                                                               in1=g[:, :], op0=ALU.add, op1=ALU.mult),
                 r=[modT, g], w=[A])
        for half in range(2):
            k.op("pe", lambda: nc.tensor.transpose(out=tpf[:, :], in_=modT[:, half * 128:(half + 1) * 128],
                                                   identity=self.identf[:, :]), r=[modT, self.identf], w=[tpf])
            k.op("dve", lambda: nc.vector.tensor_copy(out=rows[:, :], in_=tpf[:, :]), r=[tpf], w=[rows])
            k.dma("sp", self.modrows[half * 128:(half + 1) * 128, :], rows[:, :], rows, store=True)
        k.end()

    def mod_bc(self, tile, seg):
        flat = self.modrows.rearrange("a b -> (a b)")
        src = flat[seg * 4096:(seg + 1) * 4096]
        self.k.dma("sp", tile[:, :], src.partition_broadcast(128), tile)

    def phase_B(self):
        k, nc = self.k, self.nc
        k.begin()
        self.make_tp()
        wkv = k.sb([128, 32, 576], BF16, "wkv")
        wuk = k.sb([128, 4, 2048], BF16, "wuk")
        wuv = k.sb([128, 4, 2048], BF16, "wuv")
        gkv = k.sb([128, 4], F32, "gkv")
        for q4 in range(4):
            k.dma("pool", wkv[:, q4 * 8:(q4 + 1) * 8, :], self.wkv[:, q4 * 8:(q4 + 1) * 8, :], wkv)
        k.dma("pool", wuk[:, :, :], self.wuk, wuk)
        k.dma("pool", wuv[:, :, :], self.wuv, wuv)
        k.dma("sp", gkv[:, :], self.gkv, gkv)
        sin_t, cos_t = self.rope_tables(self.posall, 32, "B")
        xts = [k.sb([128, D], F32, "xt%d" % i) for i in range(2)]
        junk = k.sb([128, D], BF16, "junk")
        xn = [k.sb([128, D], BF16, "xn%d" % i) for i in range(2)]
        ssq = [k.sb([128, 1], F32, "ssq%d" % i) for i in range(2)]
        rstd = [k.sb([128, 1], F32, "rstd%d" % i) for i in range(2)]
        ssq2s = [k.sb([128, 1], F32, "ssq2%d" % i) for i in range(2)]
        rstd2s = [k.sb([128, 1], F32, "rstd2%d" % i) for i in range(2)]
        hTs = [k.sb([128, 32, 128], BF16, "hT%d" % i) for i in range(2)]
        kvns = [k.sb([128, 512], BF16, "kvn%d" % i) for i in range(2)]
        pe_sbs = [k.sb([128, 64], F32, "pe_sb%d" % i) for i in range(2)]
        kpes = [k.sb([128, 64], BF16, "kpe%d" % i) for i in range(2)]
        tas = [k.sb([128, 32], F32, "ta%d" % i) for i in range(2)]
        tbs = [k.sb([128, 32], F32, "tb%d" % i) for i in range(2)]
        kvnTs = [k.sb([128, 4, 512], BF16, "kvnT%d" % i) for i in range(2)]
        ksts = [k.sb([128, 512], BF16, "kst%d" % i) for i in range(3)]
        vsts = [k.sb([128, 2048], BF16, "vst%d" % i) for i in range(2)]
        ps_kv = k.ps([128, 512], F32, "ps_kv")
        ps_pe = k.ps([128, 64], F32, "ps_pe")
        ps_k = [k.ps([128, 512], F32, "ps_k%d" % i) for i in range(2)]
        ps_v = [k.ps([128, 512], F32, "ps_v%d" % i) for i in range(2)]
        nk = 0
        nv = 0
        bstage = 9
        for kb in range(32):
            g = kb // 4
            xt = xts[kb % 2]
            hT = hTs[kb % 2]
            kvnT = kvnTs[g % 2]
            ssq2, rstd2, kvn, pe_sb, kpe = ssq2s[kb % 2], rstd2s[kb % 2], kvns[kb % 2], pe_sbs[kb % 2], kpes[kb % 2]
            ta, tb = tas[kb % 2], tbs[kb % 2]
            k.dma("sp", xt[:, :], self.xall[kb * 128:(kb + 1) * 128, :], xt)
            if bstage < 1:
                continue
            self.norm_featmajor(xt, D, self.Amix, self.modT, hT, 0, junk, xn, ssq, rstd)
            if bstage < 2:
                continue
            for kc in range(32):
                k.op("pe", lambda: nc.tensor.matmul(ps_kv[:, :], lhsT=hT[:, kc, :], rhs=wkv[:, kc, 0:512],
                                                    start=(kc == 0), stop=(kc == 31)), r=[hT, wkv], w=[ps_kv])
                k.op("pe", lambda: nc.tensor.matmul(ps_pe[:, :], lhsT=hT[:, kc, :], rhs=wkv[:, kc, 512:576],
                                                    start=(kc == 0), stop=(kc == 31)), r=[hT, wkv], w=[ps_pe])
            if bstage < 3:
                continue
            self.sumsq(ps_kv[:, :], [ps_kv], junk[:, 0:512], junk, ssq2)
            self.rstd_from_ssq(ssq2, rstd2, 512)
            k.op("act", lambda: nc.scalar.activation(out=kvn[:, :], in_=ps_kv[:, :], func=AF.Copy,
                                                     scale=rstd2[:, 0:1]), r=[ps_kv, rstd2], w=[kvn])
            k.op("dve", lambda: nc.vector.tensor_copy(out=pe_sb[:, :], in_=ps_pe[:, :]), r=[ps_pe], w=[pe_sb])
            self.rope_apply(pe_sb[:, 0:32], pe_sb[:, 32:64], cos_t[:, kb, :], sin_t[:, kb, :],
                            kpe[:, 0:32], kpe[:, 32:64], [128, 32], [pe_sb, cos_t, sin_t], kpe, (ta, tb))
            if bstage < 4:
                continue
            for c in range(4):
                self.transpose_evac(kvn, kvn[:, c * 128:(c + 1) * 128], 128, kvnT,
                                    kvnT[:, c, (kb % 4) * 128:(kb % 4 + 1) * 128],
                                    scale_ap=gkv[:, c:c + 1], extra_r=[gkv])
            self.transpose_evac(kpe, kpe[:, :], 64, self.kpeT, self.kpeT[0:64, kb * 128:(kb + 1) * 128])
            if bstage < 5:
                continue
            if kb % 4 == 3:
                for h in range(16):
                    pk = ps_k[nk % 2]
                    kst = ksts[nk % 3]
                    nk += 1
                    for c in range(4):
                        k.op("pe", lambda: nc.tensor.matmul(pk[:, :], lhsT=wuk[:, c, h * 128:(h + 1) * 128],
                                                            rhs=kvnT[:, c, :], start=(c == 0), stop=(c == 3)),
                             r=[wuk, kvnT], w=[pk])
                    e = self.evac_engine()
                    if e == "dve":
                        k.op("dve", lambda: nc.vector.tensor_copy(out=kst[:, :], in_=pk[:, :]), r=[pk], w=[kst])
                    else:
                        k.op("act", lambda: nc.scalar.copy(out=kst[:, :], in_=pk[:, :]), r=[pk], w=[kst])
                    k.dma("sp", self.kT_d[h, :, g * 512:(g + 1) * 512], kst[:, :], kst, store=True)
                for tb4 in range(4):
                    vst = vsts[tb4 % 2]
                    for hg in range(4):
                        pv = ps_v[nv % 2]
                        nv += 1
                        for c in range(4):
                            k.op("pe", lambda: nc.tensor.matmul(pv[:, :], lhsT=kvnT[:, c, tb4 * 128:(tb4 + 1) * 128],
                                                                rhs=wuv[:, c, hg * 512:(hg + 1) * 512],
                                                                start=(c == 0), stop=(c == 3)),
                                 r=[wuv, kvnT], w=[pv])
                        e = self.evac_engine()
                        if e == "dve":
                            k.op("dve", lambda: nc.vector.tensor_copy(out=vst[:, hg * 512:(hg + 1) * 512],
                                                                      in_=pv[:, :]), r=[pv], w=[vst])
                        else:
                            k.op("act", lambda: nc.scalar.copy(out=vst[:, hg * 512:(hg + 1) * 512], in_=pv[:, :]),
                                 r=[pv], w=[vst])
                    r0 = (g * 4 + tb4) * 128
                    k.dma("sp", self.v_d[r0:r0 + 128, :], vst[:, :], vst, store=True)
        k.end()

    def phase_C(self):
        k, nc = self.k, self.nc
        k.begin()
        self.make_tp()
        hTq = k.sb([128, 32, 1024], BF16, "hTq")
        k.begin()
        xts = [k.sb([128, D], F32, "xt%d" % i) for i in range(2)]
        junk = k.sb([128, D], BF16, "junk")
        xn = [k.sb([128, D], BF16, "xn%d" % i) for i in range(2)]
        ssq = [k.sb([128, 1], F32, "ssq%d" % i) for i in range(2)]
        rstd = [k.sb([128, 1], F32, "rstd%d" % i) for i in range(2)]
        for tt in range(NT):
            xt = xts[tt % 2]
            k.dma("sp", xt[:, :], self.xq[tt * 128:(tt + 1) * 128, :], xt)
            self.norm_featmajor(xt, D, self.Amix, self.modT, hTq, tt * 128, junk, xn, ssq, rstd)
        k.end()
        k.begin()
        wps = [k.sb([128, 32, 512], BF16, "wp%d" % i) for i in range(2)]
        pss = [k.ps([128, 512], F32, "psz%d" % i) for i in range(3)]
        sts = [k.sb([128, 512], F32, "stz%d" % i) for i in range(3)]
        n = 0
        for cb in range(10):
            wp = wps[cb % 2]
            for q4 in range(4):
                k.dma("pool", wp[:, q4 * 8:(q4 + 1) * 8, :], self.wqg[cb, :, q4 * 8:(q4 + 1) * 8, :], wp)
            for tt in range(NT):
                ps = pss[n % 3]
                st = sts[n % 3]
                n += 1
                for kc in range(32):
                    k.op("pe", lambda: nc.tensor.matmul(ps[:, :], lhsT=hTq[:, kc, tt * 128:(tt + 1) * 128],
                                                        rhs=wp[:, kc, :], start=(kc == 0), stop=(kc == 31)),
                         r=[hTq, wp], w=[ps])
                if cb < 2:
                    k.op("dve", lambda: nc.vector.tensor_copy(out=st[:, :], in_=ps[:, :]), r=[ps], w=[st])
                    k.dma("sp", self.zq_d[tt * 128:(tt + 1) * 128, cb * 512:(cb + 1) * 512], st[:, :], st, store=True)
                else:
                    k.op("act", lambda: nc.scalar.activation(out=st[:, :], in_=ps[:, :], func=AF.Gelu),
                         r=[ps], w=[st])
                    k.dma("sp", self.zg_d[tt * 128:(tt + 1) * 128, (cb - 2) * 512:(cb - 1) * 512], st[:, :], st,
                          store=True)
        k.end()
        k.end()
        if self.stop_after == "C2":
            return
        k.begin()
        self.make_tp()
        wuq = k.sb([128, 8, 3072], BF16, "wuq")
        gq = k.sb([128, 8], F32, "gq")
        for q4 in range(4):
            k.dma("pool", wuq[:, q4 * 2:(q4 + 1) * 2, :], self.wuq[:, q4 * 2:(q4 + 1) * 2, :], wuq)
        k.dma("sp", gq[:, :], self.gq, gq)
        sin_t, cos_t = self.rope_tables(self.posq, NT, "C")
        qlats = [k.sb([128, 1024], F32, "qlat%d" % i) for i in range(2)]
        junk = k.sb([128, 1024], BF16, "junkq")
        qn = k.sb([128, 1024], BF16, "qn")
        ssq = k.sb([128, 1], F32, "ssq")
        rstd = k.sb([128, 1], F32, "rstd")
        qnT = k.sb([128, 8, 128], BF16, "qnT")
        q_sb = k.sb([128, 3072], F32, "q_sb")
        qb = k.sb([128, 16, 192], BF16, "qb")
        ta = k.sb([128, 16, 32], F32, "ta")
        tb = k.sb([128, 16, 32], F32, "tb")
        qTst = [k.sb([128, 16, 128], BF16, "qTst%d" % i) for i in range(2)]
        qpeTst = [k.sb([64, 16, 128], BF16, "qpeTst%d" % i) for i in range(2)]
        psq = [k.ps([128, 512], F32, "psq%d" % i) for i in range(2)]
        n = 0
        for tt in range(NT):
            ql = qlats[tt % 2]
            k.dma("sp", ql[:, :], self.zq_d[tt * 128:(tt + 1) * 128, :], ql)
            self.norm_featmajor(ql, 1024, gq, None, qnT, 0, junk, qn, ssq, rstd)
            for cb in range(6):
                ps = psq[n % 2]
                n += 1
                for c in range(8):
                    k.op("pe", lambda: nc.tensor.matmul(ps[:, :], lhsT=qnT[:, c, :],
                                                        rhs=wuq[:, c, cb * 512:(cb + 1) * 512],
                                                        start=(c == 0), stop=(c == 7)), r=[qnT, wuq], w=[ps])
                e = self.evac_engine()
                if e == "dve":
                    k.op("dve", lambda: nc.vector.tensor_copy(out=q_sb[:, cb * 512:(cb + 1) * 512], in_=ps[:, :]),
                         r=[ps], w=[q_sb])
                else:
                    k.op("act", lambda: nc.scalar.copy(out=q_sb[:, cb * 512:(cb + 1) * 512], in_=ps[:, :]),
                         r=[ps], w=[q_sb])
            qv = q_sb[:, :].rearrange("p (h d) -> p h d", h=16)
            k.op("act", lambda: nc.scalar.copy(out=qb[:, :, 0:128], in_=qv[:, :, 0:128]), r=[q_sb], w=[qb])
            cosb = cos_t[:, tt, :].unsqueeze(1).to_broadcast([128, 16, 32])
            sinb = sin_t[:, tt, :].unsqueeze(1).to_broadcast([128, 16, 32])
            self.rope_apply(qv[:, :, 128:160], qv[:, :, 160:192], cosb, sinb,
                            qb[:, :, 128:160], qb[:, :, 160:192], [128, 16, 32], [q_sb, cos_t, sin_t], qb, (ta, tb))
            qT = qTst[tt % 2]
            qpT = qpeTst[tt % 2]
            for h in range(16):
                self.transpose_evac(qb, qb[:, h, 0:128], 128, qT, qT[:, h, :])
                self.transpose_evac(qb, qb[:, h, 128:192], 64, qpT, qpT[0:64, h, :])
            k.dma("sp", self.qT_d[:, :, tt * 128:(tt + 1) * 128].rearrange("h d t -> d h t"), qT[:, :, :], qT,
                  store=True)
            k.dma("sp", self.qpeT_d[:, :, tt * 128:(tt + 1) * 128].rearrange("h d t -> d h t"), qpT[:, :, :], qpT,
                  store=True)
        k.end()
        if self.stop_after == "C3":
            return
        k.begin()
        self.make_tp()
        gs_bc = k.sb([128, 2048], F32, "gs_bc")
        wsf = k.sb([128, 16, 128], F32, "wsf")
        trl = k.sb([128, 128], F32, "trl")
        WmT = k.sb([128, 16, 128], BF16, "WmT")
        bT = k.sb([128, 16], F32, "bT")
        bgm = k.sb([128, 16], F32, "bgm")
        k.dma("sp", gs_bc[:, :], self.gsgu.to_broadcast([128, 2048]), gs_bc)
        k.dma("sp", wsf[:, :, :], self.wsgu, wsf)
        k.dma("sp", trl[:, :], self.tril, trl)
        k.dma("sp", bT[:, :], self.bsgu, bT)
        k.dma("sp", bgm[:, :], self.bgm, bgm)
        k.op("dve", lambda: nc.vector.tensor_tensor(out=WmT[:, :, :], in0=wsf[:, :, :],
                                                    in1=trl[:, :].unsqueeze(1).to_broadcast([128, 16, 128]),
                                                    op=ALU.mult), r=[wsf, trl], w=[WmT])
        us = [k.sb([128, 2048], F32, "u%d" % i) for i in range(2)]
        vs = [k.sb([128, 2048], F32, "v%d" % i) for i in range(2)]
        vc = k.sb([128, 2048], F32, "vc")
        junk = k.sb([128, 2048], BF16, "junkg")
        vn = k.sb([128, 2048], BF16, "vn")
        yg = k.sb([128, 2048], F32, "yg")
        ygn = k.sb([128, 2048], BF16, "ygn")
        msum = k.sb([128, 1], F32, "msum")
        ssq = k.sb([128, 1], F32, "ssq")
        rstd = k.sb([128, 1], F32, "rstd")
        ygT = [k.sb([128, 16, 128], BF16, "ygT%d" % i) for i in range(2)]
        svps = k.ps([128, 2048], F32, "svps")
        for tt in range(NT):
            u = us[tt % 2]
            v = vs[tt % 2]
            k.dma("sp", u[:, :], self.zg_d[tt * 128:(tt + 1) * 128, 0:2048], u)
            k.dma("sp", v[:, :], self.zg_d[tt * 128:(tt + 1) * 128, 2048:4096], v)
            k.op("dve", lambda: nc.vector.reduce_sum(out=msum[:, :], in_=v[:, :], axis=AX.X), r=[v], w=[msum])
            k.op("dve", lambda: nc.vector.tensor_scalar(out=msum[:, :], in0=msum[:, :], scalar1=-1.0 / 2048,
                                                        scalar2=None, op0=ALU.mult), r=[msum], w=[msum])
            k.op("act", lambda: nc.scalar.activation(out=vc[:, :], in_=v[:, :], func=AF.Identity,
                                                     bias=msum[:, 0:1], scale=1.0), r=[v, msum], w=[vc])
            self.sumsq(vc[:, :], [vc], junk[:, :], junk, ssq)
            self.rstd_from_ssq(ssq, rstd, 2048)
            k.op("dve", lambda: nc.vector.scalar_tensor_tensor(out=vn[:, :], in0=vc[:, :], scalar=rstd[:, 0:1],
                                                               in1=gs_bc[:, :], op0=ALU.mult, op1=ALU.mult),
                 r=[vc, rstd, gs_bc], w=[vn])
            for h in range(16):
                k.op("pe", lambda: nc.tensor.matmul(svps[:, h * 128:(h + 1) * 128], lhsT=WmT[:, h, :],
                                                    rhs=vn[:, h * 128:(h + 1) * 128], start=True, stop=True),
                     r=[WmT, vn], w=[svps])
            ygv = yg[:, :].rearrange("p (h d) -> p h d", h=16)
            k.op("dve", lambda: nc.vector.tensor_tensor(
                out=ygv, in0=svps[:, :].rearrange("p (h d) -> p h d", h=16),
                in1=bT[:, :].unsqueeze(2).to_broadcast([128, 16, 128]), op=ALU.add), r=[svps, bT], w=[yg])
            k.op("dve", lambda: nc.vector.tensor_tensor(out=yg[:, :], in0=yg[:, :], in1=u[:, :], op=ALU.mult),
                 r=[yg, u], w=[yg])
            yT = ygT[tt % 2]
            self.norm_featmajor(yg, 2048, bgm, None, yT, 0, junk, ygn, ssq, rstd)
            k.dma("sp", self.yT_d[:, :, tt * 128:(tt + 1) * 128].rearrange("c p t -> p c t"), yT[:, :, :], yT,
                  store=True)
        k.end()

    def phase_DE(self):
        k, nc = self.k, self.nc
        k.begin()
        yT = k.sb([128, 32, 1024], BF16, "yT")
        masks = k.sb([128, 32, 128], BF16, "masks")
        rstd_m = k.sb([128, NT], F32, "rstd_m")
        k.dma("sp", yT[:, 16:32, :], self.yT_d.rearrange("c p t -> p c t"), yT)
        k.begin()
        qidx_bc = k.sb([128, 1024], F32, "qidx_bc")
        pio = k.sb([128, 1], F32, "pio")
        ones_b = k.sb([128, 128], BF16, "ones_b")
        ones_f = k.sb([128, 1], F32, "ones_f")
        bmla = k.sb([128, 16], F32, "bmla")
        k.dma("sp", qidx_bc[:, :], self.qidx.to_broadcast([128, 1024]), qidx_bc)
        k.dma("sp", pio[:, :], self.piota, pio)
        k.dma("sp", bmla[:, :], self.bmla, bmla)
        k.op("pool", lambda: nc.gpsimd.memset(ones_b[:, :], 1.0), w=[ones_b])
        k.op("pool", lambda: nc.gpsimd.memset(ones_f[:, :], 1.0), w=[ones_f])
        for s in range(NT):
            for r4 in range(4):
                kb = 4 * s + r4
                k.op("dve", lambda: nc.vector.tensor_scalar(out=masks[:, s * 4 + r4, :],
                                                            in0=qidx_bc[:, s * 128:(s + 1) * 128],
                                                            scalar1=float(-kb * 128), scalar2=pio[:, 0:1],
                                                            op0=ALU.add, op1=ALU.is_ge),
                     r=[qidx_bc, pio], w=[masks])
        kTs = [k.sb([128, S], BF16, "kTh%d" % i) for i in range(2)]
        Vs = [k.sb([128, 32, 128], BF16, "Vh%d" % i) for i in range(2)]
        qTs = [k.sb([128, 1024], BF16, "qTh%d" % i) for i in range(2)]
        qpTs = [k.sb([64, 1024], BF16, "qpTh%d" % i) for i in range(2)]
        pTs = [k.sb([128, 512], BF16, "pT%d" % i) for i in range(2)]
        rsum = k.sb([128, 1024], F32, "rsum")
        On = k.sb([128, 1024], F32, "On")
        sq = k.sb([128, 1024], F32, "sq")
        sps = [k.ps([128, 512], F32, "sps%d" % i) for i in range(2)]
        opss = [k.ps([128, 128], F32, "ops%d" % i) for i in range(2)]
        sumss = [k.ps([128, 128], F32, "sums%d" % i) for i in range(2)]
        ssqps = k.ps([128, NT], F32, "ssqps")
        ssq_acc = k.sb([128, NT], F32, "ssq_acc")
        scale = 1.0 / math.sqrt(192.0)
        groups = [(h, sl, g4) for h in range(16) for sl in range(NT) for g4 in range(sl + 1)]

        def load_head(h):
            k.dma("sp", kTs[h % 2][:, :], self.kT_d[h], kTs[h % 2])
            k.dma("sp", Vs[h % 2][:, :, :],
                  self.v_d[:, h * 128:(h + 1) * 128].rearrange("(kb p) d -> p kb d", p=128), Vs[h % 2])
            k.dma("sp", qTs[h % 2][:, :], self.qT_d[h], qTs[h % 2])
            k.dma("sp", qpTs[h % 2][:, :], self.qpeT_d[h], qpTs[h % 2])

        def emit_qk(idx):
            h, sl, g4 = groups[idx]
            kT, qT, qpT = kTs[h % 2], qTs[h % 2], qpTs[h % 2]
            sp_ = sps[idx % 2]
            q0 = sl * 128
            for i4 in range(4):
                kb = g4 * 4 + i4
                k.op("pe", lambda: nc.tensor.matmul(sp_[:, i4 * 128:(i4 + 1) * 128],
                                                    lhsT=kT[:, kb * 128:(kb + 1) * 128],
                                                    rhs=qT[:, q0:q0 + 128], start=True, stop=False),
                     r=[kT, qT], w=[sp_])
                k.op("pe", lambda: nc.tensor.matmul(sp_[:, i4 * 128:(i4 + 1) * 128],
                                                    lhsT=self.kpeT[0:64, kb * 128:(kb + 1) * 128],
                                                    rhs=qpT[0:64, q0:q0 + 128], start=False, stop=True),
                     r=[self.kpeT, qpT], w=[sp_])

        def emit_rest(idx):
            h, sl, g4 = groups[idx]
            Vh = Vs[h % 2]
            sp_ = sps[idx % 2]
            pT = pTs[idx % 2]
            q0 = sl * 128
            ns = h * NT + sl
            ops = opss[ns % 2]
            sums = sumss[ns % 2]
            ng4 = sl + 1
            k.op("act", lambda: nc.scalar.activation(out=pT[:, 0:512], in_=sp_[:, :], func=AF.Exp,
                                                     scale=scale), r=[sp_], w=[pT])
            if g4 == sl:
                k.op("dve", lambda: nc.vector.tensor_tensor(
                    out=pT[:, 0:512], in0=pT[:, 0:512],
                    in1=masks[:, sl * 4:(sl + 1) * 4, :].rearrange("p a b -> p (a b)"), op=ALU.mult),
                     r=[pT, masks], w=[pT])
            for i4 in range(4):
                kb = g4 * 4 + i4
                k.op("pe", lambda: nc.tensor.matmul(ops[:, :], lhsT=Vh[:, kb, :],
                                                    rhs=pT[:, i4 * 128:(i4 + 1) * 128],
                                                    start=(kb == 0), stop=(kb == 4 * ng4 - 1)),
                     r=[Vh, pT], w=[ops])
            for i4 in range(4):
                kb = g4 * 4 + i4
                k.op("pe", lambda: nc.tensor.matmul(sums[:, :], lhsT=ones_b[:, :],
                                                    rhs=pT[:, i4 * 128:(i4 + 1) * 128],
                                                    start=(kb == 0), stop=(kb == 4 * ng4 - 1)),
                     r=[ones_b, pT], w=[sums])
            if g4 != sl:
                return
            k.op("dve", lambda: nc.vector.reciprocal(out=rsum[:, q0:q0 + 128], in_=sums[:, :]),
                 r=[sums], w=[rsum])
            k.op("dve", lambda: nc.vector.tensor_tensor(out=On[:, q0:q0 + 128], in0=ops[:, :],
                                                        in1=rsum[:, q0:q0 + 128], op=ALU.mult),
                 r=[ops, rsum], w=[On])
            if sl != NT - 1:
                return
            k.op("act", lambda: nc.scalar.activation(out=sq[:, :], in_=On[:, :], func=AF.Square), r=[On], w=[sq])
            for tt in range(NT):
                k.op("pe", lambda: nc.tensor.matmul(ssqps[:, tt:tt + 1], lhsT=sq[:, tt * 128:(tt + 1) * 128],
                                                    rhs=ones_f[:, 0:1], start=True, stop=True),
                     r=[sq, ones_f], w=[ssqps])
            if h == 0:
                k.op("dve", lambda: nc.vector.tensor_copy(out=ssq_acc[:, :], in_=ssqps[:, :]),
                     r=[ssqps], w=[ssq_acc])
            else:
                k.op("dve", lambda: nc.vector.tensor_tensor(out=ssq_acc[:, :], in0=ssq_acc[:, :], in1=ssqps[:, :],
                                                            op=ALU.add), r=[ssqps, ssq_acc], w=[ssq_acc])
            k.op("dve", lambda: nc.vector.tensor_scalar(out=yT[:, h, :], in0=On[:, :], scalar1=bmla[:, h:h + 1],
                                                        scalar2=None, op0=ALU.mult), r=[On, bmla], w=[yT])
            if h + 2 < 16:
                load_head(h + 2)

        load_head(0)
        load_head(1)
        emit_qk(0)
        for idx in range(len(groups)):
            if idx + 1 < len(groups):
                emit_qk(idx + 1)
            emit_rest(idx)
        k.op("act", lambda: nc.scalar.activation(out=rstd_m[:, :], in_=ssq_acc[:, :], func=AF.Sqrt,
                                                 scale=1.0 / 2048, bias=self.epst[:, 0:1]),
             r=[ssq_acc, self.epst], w=[rstd_m])
        k.op("dve", lambda: nc.vector.reciprocal(out=rstd_m[:, :], in_=rstd_m[:, :]), r=[rstd_m], w=[rstd_m])
        if self.debug:
            dbg_y = nc.dram_tensor("dbg_yT", [128, 16, 1024], BF16, kind="ExternalOutput").ap()
            dbg_r = nc.dram_tensor("dbg_rstd", [128, NT], F32, kind="ExternalOutput").ap()
            dbg_m = nc.dram_tensor("dbg_masks", [128, 32, 128], BF16, kind="ExternalOutput").ap()
            dbg_s = nc.dram_tensor("dbg_rsum", [128, 1024], F32, kind="ExternalOutput").ap()
            k.dma("sp", dbg_y, yT[:, 0:16, :], yT, store=True)
            k.dma("sp", dbg_r, rstd_m[:, :], rstd_m, store=True)
            k.dma("sp", dbg_m, masks[:, :, :], masks, store=True)
            k.dma("sp", dbg_s, rsum[:, :], rsum, store=True)
        k.end()
        k.begin()
        gta = k.sb([128, D], F32, "gta")
        self.mod_bc(gta, 2)
        wps = [k.sb([128, 32, 512], BF16, "wo%d" % i) for i in range(2)]
        ps1 = [k.ps([128, 512], F32, "ps1_%d" % i) for i in range(2)]
        ps2 = [k.ps([128, 512], F32, "ps2_%d" % i) for i in range(2)]
        t2s = [k.sb([128, 512], F32, "t2_%d" % i) for i in range(2)]
        t1s = [k.sb([128, 512], F32, "t1_%d" % i) for i in range(2)]
        xts = [k.sb([128, 512], F32, "xe%d" % i) for i in range(2)]
        n = 0
        for cb in range(8):
            wp = wps[cb % 2]
            for q4 in range(4):
                k.dma("pool", wp[:, q4 * 8:(q4 + 1) * 8, :], self.wout[cb, :, q4 * 8:(q4 + 1) * 8, :], wp)
            for tt in range(NT):
                p1 = ps1[n % 2]
                p2 = ps2[n % 2]
                t2 = t2s[n % 2]
                t1 = t1s[n % 2]
                xt = xts[n % 2]
                n += 1
                k.dma("sp", xt[:, :], self.xq[tt * 128:(tt + 1) * 128, cb * 512:(cb + 1) * 512], xt)
                for kc in range(16):
                    k.op("pe", lambda: nc.tensor.matmul(p1[:, :], lhsT=yT[:, kc, tt * 128:(tt + 1) * 128],
                                                        rhs=wp[:, kc, :], start=(kc == 0), stop=(kc == 15)),
                         r=[yT, wp], w=[p1])
                for kc in range(16, 32):
                    k.op("pe", lambda: nc.tensor.matmul(p2[:, :], lhsT=yT[:, kc, tt * 128:(tt + 1) * 128],
                                                        rhs=wp[:, kc, :], start=(kc == 16), stop=(kc == 31)),
                         r=[yT, wp], w=[p2])
                k.op("act", lambda: nc.scalar.copy(out=t2[:, :], in_=p2[:, :]), r=[p2], w=[t2])
                k.op("dve", lambda: nc.vector.scalar_tensor_tensor(out=t1[:, :], in0=p1[:, :],
                                                                   scalar=rstd_m[:, tt:tt + 1], in1=t2[:, :],
                                                                   op0=ALU.mult, op1=ALU.add),
                     r=[p1, rstd_m, t2], w=[t1])
                k.op("dve", lambda: nc.vector.tensor_tensor(out=t1[:, :], in0=t1[:, :],
                                                             in1=gta[:, cb * 512:(cb + 1) * 512], op=ALU.mult),
                     r=[t1, gta], w=[t1])
                k.op("dve", lambda: nc.vector.tensor_tensor(out=t1[:, :], in0=t1[:, :], in1=xt[:, :], op=ALU.add),
                     r=[t1, xt], w=[t1])
                k.dma("sp", self.x1_d[tt * 128:(tt + 1) * 128, cb * 512:(cb + 1) * 512], t1[:, :], t1, store=True)
        k.end()
        k.end()

    def phase_F(self):
        k, nc = self.k, self.nc
        for g in range(2):
            k.begin()
            Facc = k.sb([128, 4, D], F32, "Facc")
            k.begin()
            hnT = k.sb([128, 32, 512], BF16, "hnT")
            s_sb = k.sb([128, 4, 2048], F32, "s_sb")
            tau = k.sb([128, 4, 8], F32, "tau")
            gbias = k.sb([128, 4, 8], F32, "gbias")
            k.begin()
            self.make_tp()
            xts = [k.sb([128, D], F32, "xf%d" % i) for i in range(2)]
            junk = k.sb([128, D], BF16, "junk")
            xn = [k.sb([128, D], BF16, "xn%d" % i) for i in range(2)]
            ssq = [k.sb([128, 1], F32, "ssq%d" % i) for i in range(2)]
            rstd = [k.sb([128, 1], F32, "rstd%d" % i) for i in range(2)]
            for t4 in range(4):
                tt = g * 4 + t4
                xt = xts[t4 % 2]
                k.dma("sp", xt[:, :], self.x1_d[tt * 128:(tt + 1) * 128, :], xt)
                self.norm_featmajor(xt, D, self.Affn, self.modT, hnT, t4 * 128, junk, xn, ssq, rstd, boff=96)
            k.end()
            k.begin()
            keysT = k.sb([128, 16, 128], F32, "keysT")
            k.dma("sp", keysT[:, :, :], self.keysT, keysT)
            qchs = [k.sb([128, 512], F32, "qch%d" % i) for i in range(2)]
            wps = [k.sb([128, 32, 256], BF16, "wq%d" % i) for i in range(2)]
            psq = [k.ps([128, 512], F32, "psq%d" % i) for i in range(2)]
            sps4 = [k.ps([128, 512], F32, "sps4_%d" % i) for i in range(2)]
            n = 0
            for cb in range(8):
                wp = wps[cb % 2]
                for q4 in range(4):
                    k.dma("pool", wp[:, q4 * 8:(q4 + 1) * 8, :], self.wpq[cb, :, q4 * 8:(q4 + 1) * 8, :], wp)
                for c4 in range(2):
                    ch = cb * 2 + c4
                    ps = psq[n % 2]
                    qch = qchs[n % 2]
                    sp4 = sps4[n % 2]
                    n += 1
                    for kc in range(32):
                        k.op("pe", lambda: nc.tensor.matmul(ps[:, :], lhsT=wp[:, kc, c4 * 128:(c4 + 1) * 128],
                                                            rhs=hnT[:, kc, :], start=(kc == 0), stop=(kc == 31)),
                             r=[wp, hnT], w=[ps])
                    k.op("dve", lambda: nc.vector.tensor_copy(out=qch[:, :], in_=ps[:, :]), r=[ps], w=[qch])
                    for t4 in range(4):
                        k.op("pe", lambda: nc.tensor.matmul(sp4[:, t4 * 128:(t4 + 1) * 128],
                                                            lhsT=qch[:, t4 * 128:(t4 + 1) * 128],
                                                            rhs=keysT[:, ch, :], start=True, stop=True),
                             r=[qch, keysT], w=[sp4])
                    k.op("act", lambda: nc.scalar.copy(out=s_sb[:, :, ch * 128:(ch + 1) * 128],
                                                       in_=sp4[:, :].rearrange("p (t n) -> p t n", t=4)),
                         r=[sp4], w=[s_sb])
            sv = k.sb([128, 16, 16], F32, "sv")
            wk = k.sb([128, 128], F32, "wk")
            cand = k.sb([128, 8, 256], F32, "cand")
            wk2 = k.sb([128, 256], F32, "wk2")
            v24 = k.sb([128, 8, 24], F32, "v24")
            negm = k.sb([128, 8], F32, "negm")
            ez = k.sb([128, 8, 16], F32, "ez")
            zs = k.sb([128, 8], F32, "zs")
            for t4 in range(4):
                for ch in range(16):
                    sl = s_sb[:, t4, ch * 128:(ch + 1) * 128]
                    k.op("dve", lambda: nc.vector.max(out=sv[:, ch, 0:8], in_=sl), r=[s_sb], w=[sv])
                    k.op("dve", lambda: nc.vector.match_replace(out=wk[:, :], in_to_replace=sv[:, ch, 0:8],
                                                                in_values=sl, imm_value=-1e30),
                         r=[sv, s_sb], w=[wk])
                    k.op("dve", lambda: nc.vector.max(out=sv[:, ch, 8:16], in_=wk[:, :]), r=[wk], w=[sv])
                svv = sv[:, :, :].rearrange("p (h two) a -> p h two a", two=2)
                k.op("dve", lambda: nc.vector.tensor_tensor(
                    out=cand[:, :, :].rearrange("p h (a b) -> p h a b", a=16),
                    in0=svv[:, :, 0, :].unsqueeze(3).to_broadcast([128, 8, 16, 16]),
                    in1=svv[:, :, 1, :].unsqueeze(2).to_broadcast([128, 8, 16, 16]), op=ALU.add),
                     r=[sv], w=[cand])
                for h in range(8):
                    k.op("dve", lambda: nc.vector.max(out=v24[:, h, 0:8], in_=cand[:, h, :]), r=[cand], w=[v24])
                    k.op("dve", lambda: nc.vector.match_replace(out=wk2[:, :], in_to_replace=v24[:, h, 0:8],
                                                                in_values=cand[:, h, :], imm_value=-1e30),
                         r=[v24, cand], w=[wk2])
                    k.op("dve", lambda: nc.vector.max(out=v24[:, h, 8:16], in_=wk2[:, :]), r=[wk2], w=[v24])
                    k.op("dve", lambda: nc.vector.match_replace(out=wk2[:, :], in_to_replace=v24[:, h, 8:16],
                                                                in_values=wk2[:, :], imm_value=-1e30),
                         r=[v24, wk2], w=[wk2])
                    k.op("dve", lambda: nc.vector.max(out=v24[:, h, 16:24], in_=wk2[:, :]), r=[wk2], w=[v24])
                k.op("dve", lambda: nc.vector.tensor_tensor(out=tau[:, t4, :], in0=v24[:, :, 15], in1=v24[:, :, 16],
                                                            op=ALU.add), r=[v24], w=[tau])
                k.op("dve", lambda: nc.vector.tensor_scalar(out=tau[:, t4, :], in0=tau[:, t4, :], scalar1=0.5,
                                                            scalar2=None, op0=ALU.mult), r=[tau], w=[tau])
                k.op("dve", lambda: nc.vector.tensor_scalar(out=negm[:, :], in0=v24[:, :, 0], scalar1=-1.0,
                                                            scalar2=None, op0=ALU.mult), r=[v24], w=[negm])
                k.op("dve", lambda: nc.vector.tensor_tensor(out=ez[:, :, :], in0=v24[:, :, 0:16],
                                                            in1=negm[:, :].unsqueeze(2).to_broadcast([128, 8, 16]),
                                                            op=ALU.add), r=[v24, negm], w=[ez])
                k.op("act", lambda: nc.scalar.activation(out=ez[:, :, :], in_=ez[:, :, :], func=AF.Exp),
                     r=[ez], w=[ez])
                k.op("dve", lambda: nc.vector.reduce_sum(out=zs[:, :], in_=ez[:, :, :], axis=AX.X), r=[ez], w=[zs])
                k.op("act", lambda: nc.scalar.activation(out=zs[:, :], in_=zs[:, :], func=AF.Ln), r=[zs], w=[zs])
                k.op("dve", lambda: nc.vector.tensor_tensor(out=gbias[:, t4, :], in0=negm[:, :], in1=zs[:, :],
                                                            op=ALU.subtract), r=[negm, zs], w=[gbias])
            k.end()
            if self.stop_after == "F2":
                k.end()
                k.end()
                return
            k.begin()
            Us = [k.sb([128, 32, 128], BF16, "U%d" % i) for i in range(2)]
            Vp = [k.sb([128, 8, 512], BF16, "V%d" % i) for i in range(2)]
            gat = k.sb([128, 8, 512], BF16, "GAT")
            ATs = [k.sb([128, 512], BF16, "AT%d" % i) for i in range(2)]
            NB = 2
            cs = [k.sb([128, 8, 128], F32, "cs%d" % i) for i in range(NB)]
            Es = [k.sb([128, 8, 128], BF16, "Es%d" % i) for i in range(NB)]
            Ghs = [k.sb([128, 8, 128], BF16, "Gh%d" % i) for i in range(NB)]
            Gacc = [[k.sb([128, 8, 128], BF16, "Gacc%d_%d" % (j, i)) for i in range(4)] for j in range(2)]
            psA = [k.ps([128, 512], F32, "psA%d" % i) for i in range(2)]
            psF = [k.ps([128, 512], F32, "psF%d" % i) for i in range(4)]
            psG = [k.ps([128, 512], F32, "psG%d" % i) for i in range(2)]
            st = {"ng": 0, "na": 0, "nf": 0, "nt": 0, "issued": 0, "nU": 0, "nV": 0}
            pieces = []
            for eb in range(16):
                pieces += [("U", eb, c) for c in range(8)]
                pieces += [("V", eb, db) for db in range(8)]
            bufmap = {}

            def ensure(i):
                while st["issued"] <= min(i, len(pieces) - 1):
                    kind, eb_, j = pieces[st["issued"]]
                    if kind == "U":
                        t = Us[st["nU"] % 2]
                        st["nU"] += 1
                        k.dma("pool", t[:, :, :], self.U[eb_ * 8 + j], t)
                    else:
                        t = Vp[st["nV"] % 2]
                        st["nV"] += 1
                        k.dma("pool", t[:, :, :], self.V[eb_, j], t)
                    bufmap[st["issued"]] = t
                    st["issued"] += 1

            def emit_G(eb, part):
                t4 = part // 4
                ga = Gacc[eb % 2][t4]
                for h in range((part % 4) * 2, (part % 4) * 2 + 2):
                    cc = cs[st["ng"] % NB]
                    Ee = Es[st["ng"] % NB]
                    Gh = Ghs[st["ng"] % NB]
                    st["ng"] += 1
                    s1 = s_sb[:, t4, (2 * h) * 128 + eb * 8:(2 * h) * 128 + eb * 8 + 8]
                    s2 = s_sb[:, t4, (2 * h + 1) * 128:(2 * h + 2) * 128]
                    k.op("pool", lambda: nc.gpsimd.tensor_tensor(
                        out=cc[:, :, :], in0=s1.unsqueeze(2).to_broadcast([128, 8, 128]),
                        in1=s2.unsqueeze(1).to_broadcast([128, 8, 128]), op=ALU.add), r=[s_sb], w=[cc])
                    k.op("act", lambda: nc.scalar.activation(out=Ee[:, :, :], in_=cc[:, :, :], func=AF.Exp,
                                                             bias=gbias[:, t4, h:h + 1], scale=1.0),
                         r=[cc, gbias], w=[Ee])
                    dst = ga if h == 0 else Gh
                    k.op("dve", lambda: nc.vector.scalar_tensor_tensor(
                        out=dst[:, :, :], in0=cc[:, :, :], scalar=tau[:, t4, h:h + 1], in1=Ee[:, :, :],
                        op0=ALU.is_ge, op1=ALU.mult), r=[cc, tau, Ee], w=[dst])
                    if h > 0:
                        k.op("dve", lambda: nc.vector.tensor_tensor(out=ga[:, :, :], in0=ga[:, :, :],
                                                                    in1=Gh[:, :, :], op=ALU.add),
                             r=[ga, Gh], w=[ga])

            for part in range(16):
                emit_G(0, part)
            pi = 0
            for eb in range(16):
                for c in range(8):
                    ensure(pi + 1)
                    Ut = bufmap.pop(pi)
                    pi += 1
                    pa = psA[st["na"] % 2]
                    at = ATs[st["na"] % 2]
                    st["na"] += 1
                    for kc in range(32):
                        k.op("pe", lambda: nc.tensor.matmul(pa[:, :], lhsT=Ut[:, kc, :], rhs=hnT[:, kc, :],
                                                            start=(kc == 0), stop=(kc == 31)), r=[Ut, hnT], w=[pa])
                    k.op("act", lambda: nc.scalar.activation(out=at[:, :], in_=pa[:, :], func=AF.Gelu),
                         r=[pa], w=[at])
                    pg = psG[st["nt"] % 2]
                    st["nt"] += 1
                    for t4 in range(4):
                        ga = Gacc[eb % 2][t4]
                        k.op("pe", lambda: nc.tensor.matmul(pg[:, t4 * 128:(t4 + 1) * 128], lhsT=ga[:, c, :],
                                                            rhs=self.identb[:, :], start=True, stop=True),
                             r=[ga, self.identb], w=[pg])
                    k.op("dve", lambda: nc.vector.tensor_tensor(out=gat[:, c, :], in0=pg[:, :], in1=at[:, :],
                                                                op=ALU.mult), r=[pg, at], w=[gat])
                    if eb + 1 < 16:
                        emit_G(eb + 1, c)
                for db in range(8):
                    ensure(pi + 1)
                    Vt = bufmap.pop(pi)
                    pi += 1
                    for t4 in range(4):
                        pf = psF[st["nf"] % 4]
                        st["nf"] += 1
                        for c in range(8):
                            k.op("pe", lambda: nc.tensor.matmul(pf[:, :], lhsT=gat[:, c, t4 * 128:(t4 + 1) * 128],
                                                                rhs=Vt[:, c, :], start=(c == 0), stop=(c == 7)),
                                 r=[gat, Vt], w=[pf])
                        fa = Facc[:, t4, db * 512:(db + 1) * 512]
                        if eb == 0:
                            k.op("dve", lambda: nc.vector.tensor_copy(out=fa, in_=pf[:, :]), r=[pf], w=[Facc])
                        else:
                            k.op("dve", lambda: nc.vector.tensor_tensor(out=fa, in0=fa, in1=pf[:, :], op=ALU.add),
                                 r=[pf, Facc], w=[Facc])
                    if eb + 1 < 16:
                        emit_G(eb + 1, 8 + db)
            k.end()
            k.end()
            k.begin()
            gtf = k.sb([128, D], F32, "gtf")
            Afin = k.sb([128, D], F32, "Afin")
            Bfin = k.sb([128, D], F32, "Bfin")
            gfb = k.sb([128, D], F32, "gfb")
            self.mod_bc(gtf, 5)
            self.mod_bc(Bfin, 6)
            self.mod_bc(Afin, 7)
            k.dma("sp", gfb[:, :], self.gfin.to_broadcast([128, D]), gfb)
            k.op("dve", lambda: nc.vector.scalar_tensor_tensor(out=Afin[:, :], in0=Afin[:, :], scalar=1.0,
                                                               in1=gfb[:, :], op0=ALU.add, op1=ALU.mult),
                 r=[Afin, gfb], w=[Afin])
            x1s = [k.sb([128, D], F32, "x1_%d" % i) for i in range(2)]
            junk = gfb
            ssq = k.sb([128, 1], F32, "ssq")
            rstd = k.sb([128, 1], F32, "rstd")
            for t4 in range(4):
                tt = g * 4 + t4
                x1 = x1s[t4 % 2]
                k.dma("sp", x1[:, :], self.x1_d[tt * 128:(tt + 1) * 128, :], x1)
                k.op("dve", lambda: nc.vector.tensor_tensor(out=Facc[:, t4, :], in0=Facc[:, t4, :], in1=gtf[:, :],
                                                             op=ALU.mult), r=[Facc, gtf], w=[Facc])
                k.op("dve", lambda: nc.vector.tensor_tensor(out=x1[:, :], in0=x1[:, :], in1=Facc[:, t4, :],
                                                            op=ALU.add), r=[x1, Facc], w=[x1])
                self.sumsq(x1[:, :], [x1], junk[:, :], junk, ssq)
                self.rstd_from_ssq(ssq, rstd, D)
                k.op("dve", lambda: nc.vector.scalar_tensor_tensor(out=x1[:, :], in0=x1[:, :], scalar=rstd[:, 0:1],
                                                                   in1=Afin[:, :], op0=ALU.mult, op1=ALU.mult),
                     r=[x1, rstd, Afin], w=[x1])
                k.op("dve", lambda: nc.vector.tensor_tensor(out=x1[:, :], in0=x1[:, :], in1=Bfin[:, :], op=ALU.add),
                     r=[x1, Bfin], w=[x1])
                k.dma("sp", self.out[tt * 128:(tt + 1) * 128, :], x1[:, :], x1, store=True)
            k.end()
            k.end()

    def build(self):
        self.declare()
        self.phase_setup()
        sa = self.stop_after
        self.phase_A()
        if sa == "A":
            self.k.finish()
            return self.nc
        self.k.begin()
        self.kpeT = self.k.sb([64, S], BF16, "kpeT")
        self.phase_B()
        if sa != "B":
            self.phase_C()
            if sa not in ("C2", "C3", "C"):
                self.phase_DE()
        self.k.end()
        if sa in ("B", "C2", "C3", "C", "DE"):
            self.k.finish()
            return self.nc
        self.phase_F()
        self.k.finish()
        return self.nc


def _fm(v, n):
    return np.ascontiguousarray(np.asarray(v, np.float32).reshape(n, 128).T)


def _blocks(w, ncol):
    K, C = w.shape
    return np.ascontiguousarray(w.reshape(K // 128, 128, C // ncol, ncol).transpose(2, 1, 0, 3))


def _own_blocks(j):
    blks = []
    for m in range(4):
        blks.append(8 * m + j)
        blks.append(8 * m + 7 - j)
    return blks


_PROG = None


def _shared_inputs(w_ada, b_ada, g_norm_mix, w_in, g_q, w_uq, g_kv, w_ukv, g_sgu, w_sgu, b_sgu, beta_mla,
                   beta_gmlp, w_out, g_norm_ffn, w_pq, peer_keys, expert_u, expert_v, w_ada_f, b_ada_f, g_norm_f):
    f = np.float32
    sh = {}
    wa = np.concatenate([np.asarray(w_ada[0], f), np.asarray(w_ada_f, f)], axis=1)
    sh["wada"] = _blocks(wa, 256)
    del wa
    ba = np.concatenate([np.asarray(b_ada[0], f), np.asarray(b_ada_f, f)])
    sh["bada"] = _fm(ba, 256)
    sh["gmix"] = _fm(g_norm_mix[0], 32)
    sh["gffn"] = _fm(g_norm_ffn[0], 32)
    sh["gfin"] = np.asarray(g_norm_f, f).reshape(1, D)
    wi = np.asarray(w_in[0], f)
    sh["wkv"] = np.ascontiguousarray(wi[:, 1024:1600].reshape(32, 128, 576).transpose(1, 0, 2))
    sh["wqg"] = _blocks(np.concatenate([wi[:, 0:1024], wi[:, 1600:5696]], axis=1), 512)
    sh["gq"] = _fm(g_q[0], 8)
    sh["gkv"] = _fm(g_kv[0], 4)
    sh["wuq"] = np.ascontiguousarray(np.asarray(w_uq[0], f).reshape(8, 128, 3072).transpose(1, 0, 2))
    wk = np.asarray(w_ukv[0], f).reshape(512, 16, 256)
    sh["wuk"] = np.ascontiguousarray(wk[:, :, 0:128].reshape(4, 128, 2048).transpose(1, 0, 2))
    sh["wuv"] = np.ascontiguousarray(wk[:, :, 128:256].reshape(4, 128, 2048).transpose(1, 0, 2))
    sh["gsgu"] = np.asarray(g_sgu[0], f).reshape(1, 2048)
    sh["wsgu"] = np.ascontiguousarray(np.asarray(w_sgu[0], f).transpose(2, 0, 1))
    sh["bsgu"] = np.ascontiguousarray(np.asarray(b_sgu[0], f).T)
    sh["bmla"] = _fm(beta_mla[0], 16)
    sh["bgm"] = _fm(beta_gmlp[0], 16)
    sh["wout"] = _blocks(np.asarray(w_out[0], f), 512)
    sh["wpq"] = _blocks(np.asarray(w_pq[0], f), 256)
    pk = np.asarray(peer_keys[0], f)
    sh["keysT"] = np.ascontiguousarray(pk.reshape(16, 128, 128).transpose(2, 0, 1))
    eu = np.asarray(expert_u[0], f)
    sh["U"] = np.ascontiguousarray(eu.reshape(128, 128, 32, 128).transpose(0, 3, 2, 1))
    ev = np.asarray(expert_v[0], f)
    sh["V"] = np.ascontiguousarray(ev.reshape(16, 8, 128, 8, 512).transpose(0, 3, 2, 1, 4))
    sh["invf"] = (np.float32(10000.0) ** (-np.arange(0, 64, 2, dtype=np.float32) / np.float32(64))).astype(
        f).reshape(1, 32)
    sh["piota"] = np.arange(128, dtype=f).reshape(128, 1)
    sh["ident"] = np.eye(128, dtype=f)
    sh["tril"] = np.triu(np.ones((128, 128), f))
    return sh


def _core_inputs(core, x, c, positions):
    f = np.float32
    b, j = core // 4, core % 4
    blks = _own_blocks(j)
    xb = np.asarray(x[b], f)
    rows = np.concatenate([np.arange(bl * 128, (bl + 1) * 128) for bl in blks])
    pos = np.asarray(positions[b], np.int32)
    m = {}
    m["xq"] = np.ascontiguousarray(xb[rows])
    m["xall"] = xb
    m["posq"] = np.ascontiguousarray(pos[rows].reshape(NT, 128).T)
    m["posall"] = np.ascontiguousarray(pos.reshape(32, 128).T)
    m["qidx"] = rows.astype(f).reshape(1, 1024)
    m["cvec"] = _fm(c[b], 32)
    return m, rows


def kernel(x, c, positions, w_ada, b_ada, g_norm_mix, w_in, g_q, w_uq, g_kv, w_ukv, g_sgu, w_sgu, b_sgu,
           beta_mla, beta_gmlp, w_out, g_norm_ffn, w_pq, peer_keys, expert_u, expert_v, w_ada_f, b_ada_f,
           g_norm_f):
    global _PROG
    x = np.asarray(x)
    shared = _shared_inputs(w_ada, b_ada, g_norm_mix, w_in, g_q, w_uq, g_kv, w_ukv, g_sgu, w_sgu, b_sgu,
                            beta_mla, beta_gmlp, w_out, g_norm_ffn, w_pq, peer_keys, expert_u, expert_v,
                            w_ada_f, b_ada_f, g_norm_f)
    in_maps = []
    rows_all = []
    for core in range(8):
        m, rows = _core_inputs(core, x, c, positions)
        m.update(shared)
        in_maps.append(m)
        rows_all.append(rows)
    nc = Prog().build()
    res = run_bass_kernel_spmd(nc, in_maps, core_ids=list(range(8)))
    out = np.empty((2, S, D), np.float32)
    for core in range(8):
        out[core // 4, rows_all[core], :] = res.results[core]["out"]
    return out
```
